# Optimizing a Trainium2 kernel written in Bass

```python
import math
import jax, jax.numpy as jnp
from jax import lax
import numpy as np

D_MODEL = 1024
BATCH = 4
SEQ = 8192
DEPTH = 1

CHUNK = 64
W_LRU = D_MODEL
LRU_HEADS = 16
LRU_HEAD_DIM = W_LRU // LRU_HEADS
CONV_WIDTH = 4
LRU_C = 8.0
W_POOL = D_MODEL
POOL_WINDOWS = (2, 4, 8, 16)
POOL_GROUPS = len(POOL_WINDOWS)
POOL_GROUP_DIM = W_POOL // POOL_GROUPS
LN_EPS = 1e-5
DEEPNORM_ALPHA = (2.0 * DEPTH) ** 0.25
DEEPNORM_BETA = (8.0 * DEPTH) ** -0.25
IN_WIDTH = 2 * W_LRU + 2 * W_POOL + 2 * D_MODEL

kernel_name = "hybrid_rglru_multiscale_pool_gated"


def _layer_norm(x, g, b):
    xf = x.astype(jnp.float32)
    mu = jnp.mean(xf, axis=-1, keepdims=True)
    var = jnp.mean(jnp.square(xf - mu), axis=-1, keepdims=True)
    return ((xf - mu) * lax.rsqrt(var + LN_EPS) * g.astype(jnp.float32) + b.astype(jnp.float32)).astype(x.dtype)


def _causal_depthwise_conv(x, w, b):
    k_taps = w.shape[0]
    s = x.shape[1]
    xp = jnp.pad(x, ((0, 0), (k_taps - 1, 0), (0, 0)))
    y = xp[:, 0:s] * w[0]
    for k in range(1, k_taps):
        y = y + xp[:, k:k + s] * w[k]
    return y + b


def _rg_lru(x, w_r, b_r, w_i, b_i, lam):
    bsz, s, w = x.shape
    xh = x.reshape(bsz, s, LRU_HEADS, LRU_HEAD_DIM)
    r = jax.nn.sigmoid((jnp.einsum('bshd,hde->bshe', xh, w_r).reshape(bsz, s, w) + b_r).astype(jnp.float32))
    i = jax.nn.sigmoid((jnp.einsum('bshd,hde->bshe', xh, w_i).reshape(bsz, s, w) + b_i).astype(jnp.float32))
    log_a = -LRU_C * r * jax.nn.softplus(-lam.astype(jnp.float32))
    a = jnp.exp(log_a)
    mult = jnp.sqrt(-jnp.expm1(2.0 * log_a))
    u = mult * i * x.astype(jnp.float32)

    def step(h, inp):
        a_t, u_t = inp
        h = a_t * h + u_t
        return h, h

    h0 = jnp.zeros((bsz, w), jnp.float32)
    _, hs = lax.scan(step, h0, (jnp.swapaxes(a, 0, 1), jnp.swapaxes(u, 0, 1)))
    return jnp.swapaxes(hs, 0, 1).astype(x.dtype)


def _multiscale_pool(x, w_pool, b_pool, scale):
    bsz, s, w = x.shape
    xf = x.astype(jnp.float32)
    cs = jnp.cumsum(xf, axis=1)
    t = jnp.arange(s)
    outs = []
    for g, win in enumerate(POOL_WINDOWS):
        sl = slice(g * POOL_GROUP_DIM, (g + 1) * POOL_GROUP_DIM)
        c = cs[..., sl]
        shifted = jnp.pad(c, ((0, 0), (win, 0), (0, 0)))[:, :s]
        count = jnp.minimum(t + 1, win).astype(jnp.float32)[None, :, None]
        outs.append((c - shifted) / count - xf[..., sl])
    mixed = jnp.stack(outs, axis=2)
    y = jnp.einsum('bsgc,gcd->bsgd', mixed, w_pool.astype(jnp.float32)).reshape(bsz, s, w)
    y = (y + b_pool.astype(jnp.float32)) * scale.astype(jnp.float32)
    return y.astype(x.dtype)


def setup_inputs(seed: int = 0) -> dict:
    key = jax.random.key(seed)
    ks = jax.random.split(key, 20)
    f32 = jnp.float32
    x = jax.random.normal(ks[0], (BATCH, SEQ, D_MODEL), f32)
    w_in = jax.random.normal(ks[1], (DEPTH, D_MODEL, IN_WIDTH), f32) * D_MODEL ** -0.5
    b_in = 0.01 * jax.random.normal(ks[2], (DEPTH, IN_WIDTH), f32)
    conv_w = jax.random.normal(ks[3], (DEPTH, CONV_WIDTH, W_LRU), f32) * CONV_WIDTH ** -0.5
    conv_b = 0.01 * jax.random.normal(ks[4], (DEPTH, W_LRU), f32)
    w_rgate = jax.random.normal(ks[5], (DEPTH, LRU_HEADS, LRU_HEAD_DIM, LRU_HEAD_DIM), f32) * LRU_HEAD_DIM ** -0.5
    b_rgate = 0.01 * jax.random.normal(ks[6], (DEPTH, W_LRU), f32)
    w_igate = jax.random.normal(ks[7], (DEPTH, LRU_HEADS, LRU_HEAD_DIM, LRU_HEAD_DIM), f32) * LRU_HEAD_DIM ** -0.5
    b_igate = 0.01 * jax.random.normal(ks[8], (DEPTH, W_LRU), f32)
    a_c = jax.random.uniform(ks[9], (DEPTH, W_LRU), f32, 0.9, 0.999)
    sig = a_c ** (1.0 / LRU_C)
    lru_lambda = jnp.log(sig) - jnp.log1p(-sig)
    w_pool = jax.random.normal(ks[10], (DEPTH, POOL_GROUPS, POOL_GROUP_DIM, POOL_GROUP_DIM), f32) * POOL_GROUP_DIM ** -0.5
    b_pool = 0.01 * jax.random.normal(ks[11], (DEPTH, W_POOL), f32)
    pool_scale = 1.0 + 0.1 * jax.random.normal(ks[12], (DEPTH, W_POOL), f32)
    w_proj_a = jax.random.normal(ks[13], (DEPTH, W_LRU, D_MODEL), f32) * (W_LRU ** -0.5 * DEEPNORM_BETA)
    w_proj_b = jax.random.normal(ks[14], (DEPTH, W_POOL, D_MODEL), f32) * (W_POOL ** -0.5 * DEEPNORM_BETA)
    w_out = jax.random.normal(ks[15], (DEPTH, D_MODEL, D_MODEL), f32) * (D_MODEL ** -0.5 * DEEPNORM_BETA)
    b_out = 0.01 * jax.random.normal(ks[16], (DEPTH, D_MODEL), f32)
    ln_gain = 1.0 + 0.02 * jax.random.normal(ks[17], (DEPTH, D_MODEL), f32)
    ln_bias = 0.02 * jax.random.normal(ks[18], (DEPTH, D_MODEL), f32)
    return {"x": x, "w_in": w_in, "b_in": b_in, "conv_w": conv_w, "conv_b": conv_b,
            "w_rgate": w_rgate, "b_rgate": b_rgate, "w_igate": w_igate, "b_igate": b_igate,
            "lru_lambda": lru_lambda, "w_pool": w_pool, "b_pool": b_pool, "pool_scale": pool_scale,
            "w_proj_a": w_proj_a, "w_proj_b": w_proj_b, "w_out": w_out, "b_out": b_out,
            "ln_gain": ln_gain, "ln_bias": ln_bias}


def reference(x, w_in, b_in, conv_w, conv_b, w_rgate, b_rgate, w_igate, b_igate,
              lru_lambda, w_pool, b_pool, pool_scale, w_proj_a, w_proj_b, w_out, b_out,
              ln_gain, ln_bias):
    splits = np.cumsum([W_LRU, W_LRU, W_POOL, W_POOL, D_MODEL])
    for l in range(DEPTH):
        h = jnp.einsum('bsd,de->bse', x, w_in[l]) + b_in[l]
        xa, za, xb, zb, ga, gb = jnp.split(h, splits, axis=-1)
        xa = _causal_depthwise_conv(xa, conv_w[l], conv_b[l])
        ya = _rg_lru(xa, w_rgate[l], b_rgate[l], w_igate[l], b_igate[l], lru_lambda[l]) * jax.nn.silu(za)
        yb = _multiscale_pool(xb, w_pool[l], b_pool[l], pool_scale[l]) * jax.nn.silu(zb)
        merged = (jax.nn.sigmoid(ga) * jnp.einsum('bsw,wd->bsd', ya, w_proj_a[l])
                  + jax.nn.sigmoid(gb) * jnp.einsum('bsw,wd->bsd', yb, w_proj_b[l]))
        y = jnp.einsum('bsd,de->bse', merged, w_out[l]) + b_out[l]
        x = _layer_norm(DEEPNORM_ALPHA * x + y, ln_gain[l], ln_bias[l])
    return x
```

```python
from contextlib import ExitStack

import numpy as np

import concourse.bass as bass
import concourse.mybir as mybir
from concourse.bass_utils import run_bass_kernel_spmd

F32 = mybir.dt.float32
BF16 = mybir.dt.bfloat16
AF = mybir.ActivationFunctionType
ALU = mybir.AluOpType

D = 1024
NCH = 8
T = 512
HX = 3
HB = 16
POOL_WINDOWS = (2, 4, 8, 16)
LN_EPS = 1e-5
ALPHA = 2.0 ** 0.25
NBLK = 56
RING = 6
NGEN = 14

V_BIN = 0
V_CW = 48
V_CB = 80
V_BR = 88
V_BI = 96
V_LAM = 104
V_BP = 112
V_PS = 120
NV = 128
D_HBZA = 0
D_HBZB = 8
D_HBGA = 16
D_HBGB = 24
D_HBR = 32
D_HBI = 40
D_NSP4 = 48
D_NSP8 = 56
D_SC2 = 64
D_BP2 = 72
D_BXBF = 80
ND = 88


def c_block_order():
    order = [(0, 0), (0, 1), (1, 0), (1, 1)]
    for d in range(NCH):
        order += [(d, 2), (d, 3)]
        if d + 2 < NCH:
            order += [(d + 2, 0), (d + 2, 1)]
    return order


C_ORDER = c_block_order()
C_POS = {dk: 24 + i for i, dk in enumerate(C_ORDER)}


class Buf:
    def __init__(self, name):
        self.name = name
        self.w = None
        self.r = []
        self.live = False


class Eng:
    def __init__(self, name, sem):
        self.name = name
        self.sem = sem
        self.n = 0
        self.ops = []
        self.known = {}


class DSem:
    def __init__(self, name, sem):
        self.name = name
        self.sem = sem
        self.count = 0


class Prog:
    def __init__(self):
        self.clock = {}
        self.sems = {}

    def _merge(self, a, b):
        for k, v in b.items():
            if a.get(k, 0) < v:
                a[k] = v

    def _deps(self, reads, writes):
        deps = []
        for b in reads:
            if b.w is not None:
                deps.append(b.w)
        for b in writes:
            if b.w is not None:
                deps.append(b.w)
            deps.extend(b.r)
        return deps

    def _waits(self, eng, deps, skip_own=False):
        for tok in deps:
            s, v = tok
            if skip_own and s == eng.name:
                continue
            if eng.known.get(s, 0) >= v:
                continue
            eng.ops.append(("wait", self.sems[s], v))
            self._merge(eng.known, self.clock[tok])

    def _post(self, tok, reads, writes):
        for b in reads:
            b.r.append(tok)
        for b in writes:
            b.w = tok
            b.r = []

    def op(self, eng, fn, reads=(), writes=(), skip_own=False):
        self._waits(eng, self._deps(reads, writes), skip_own)
        eng.n += 1
        tok = (eng.name, eng.n)
        c = dict(eng.known)
        c[eng.name] = eng.n
        self.clock[tok] = c
        eng.ops.append(("op", fn, eng.sem, 1))
        self._post(tok, reads, writes)
        return tok

    def quiet(self, eng, fn, reads=(), writes=()):
        self._waits(eng, self._deps(reads, writes), True)
        eng.ops.append(("op", fn, None, 0))

    def dma(self, eng, dsem, fn, reads=(), writes=()):
        self._waits(eng, self._deps(reads, writes))
        dsem.count += 1
        tok = (dsem.name, 16 * dsem.count)
        c = dict(eng.known)
        c[dsem.name] = 16 * dsem.count
        self.clock[tok] = c
        eng.ops.append(("op", fn, dsem.sem, 16))
        self._post(tok, reads, writes)
        return tok

    def wait_tok(self, eng, tok):
        self._waits(eng, [tok])


def replay(eng, h):
    for o in eng.ops:
        if o[0] == "wait":
            h.wait_ge(o[1], o[2])
        else:
            ins = o[1](h)
            if o[2] is not None:
                ins.then_inc(o[2], o[3])


def build_program(nt_pre, nt):
    nc = bass.Bass("TRN2", target_bir_lowering=False)
    ntok = nt * T

    def din(name, shape, dt=F32):
        return nc.dram_tensor(name, list(shape), dt, kind="ExternalInput").ap()

    xpreT = din("xpreT", [max(nt_pre, 1), 128, NCH, T])
    xmainT = din("xmainT", [nt, 128, NCH, T])
    xhaloT = din("xhaloT", [128, NCH, HB])
    xtok = din("xtok", [ntok, D])
    flag_d = din("flag", [128, 1])
    invc_d = din("invc", [128, NCH, HB])
    wxa_d = din("wxa", [128, NCH, NCH, 128])
    wst_d = din("wst", [NBLK, 128, NCH, 128])
    wg_d = din("wg", [128, NCH, 2, 128])
    wp_d = din("wp", [128, 4, 2, 256])
    wo_d = din("wo", [128, NCH, D])
    vecs_d = din("vecs", [128, NV])
    bc3_d = din("bc3", [128, 3, D])
    wscr = nc.dram_tensor("wscr", [NBLK, 128, NCH, 128], BF16, kind="Internal").ap()
    out_d = nc.dram_tensor("out", [ntok, D], F32, kind="ExternalOutput").ap()

    es = ExitStack()
    with es:
        def sb(name, shape, dt=F32):
            return es.enter_context(nc.sbuf_tensor(name, list(shape), dt))

        def sem(name):
            return es.enter_context(nc.semaphore(name))

        wxa = sb("wxa_s", [128, NCH, NCH, 128], BF16)
        wg = sb("wg_s", [128, NCH, 2, 128], BF16)
        wp = sb("wp_s", [128, 4, 2, 256], BF16)
        wo = sb("wo_s", [128, NCH, D], BF16)
        vecs = sb("vecs_s", [128, NV])
        der = sb("der_s", [128, ND])
        tmpv = [sb(f"tmpv{i}", [128, NCH]) for i in range(6)]
        bc3 = sb("bc3_s", [128, 3, D])
        flag = sb("flag_s", [128, 1])
        invc = sb("invc_s", [128, NCH, HB])
        xhalo = sb("xhalo_s", [128, NCH, HB], BF16)
        ring = [sb(f"ring{i}", [128, NCH, 128], BF16) for i in range(RING)]
        xT = [sb(f"xT{i}", [128, NCH, T], BF16) for i in range(2)]
        ya = sb("ya_s", [128, NCH, T], BF16)
        yb = sb("yb_s", [128, NCH, T], BF16)
        mrg = sb("mrg_s", [128, NCH, T], BF16)
        a_s = sb("a_s", [128, NCH, T])
        a2m = sb("a2m_s", [128, NCH, T])
        w_s = sb("w_s", [128, NCH, T])
        carry = sb("carry_s", [128, NCH, HX])
        bcarry = sb("bcarry_s", [128, NCH, HB])
        hstate = sb("hstate_s", [128, NCH])
        zcol = sb("zcol_s", [128, 2])
        gen = [sb(f"gen{i}", [128, HB + T]) for i in range(NGEN)]
        xcbf = [sb(f"xcbf{i}", [128, T], BF16) for i in range(4)]
        mixbf = [sb(f"mixbf{i}", [128, 2, T], BF16) for i in range(2)]
        xtk = [sb(f"xtk{i}", [128, D]) for i in range(4)]
        stat = [sb(f"stat{i}", [128, 8]) for i in range(4)]

        psA = [es.enter_context(nc.psum_tensor(f"psA{i}", [128, T], F32)) for i in range(4)]
        psD = [es.enter_context(nc.psum_tensor(f"psD{i}", [128, T], F32)) for i in range(2)]
        psO = [es.enter_context(nc.psum_tensor(f"psO{i}", [128, T], F32)) for i in range(2)]

        P = Prog()
        act = Eng("act", sem("s_act"))
        dve = Eng("dve", sem("s_dve"))
        pool = Eng("pool", sem("s_pool"))
        pe = Eng("pe", sem("s_pe"))
        sp = Eng("sp", None)
        for e in (act, dve, pool, pe):
            P.sems[e.name] = e.sem

        def dsem(name):
            d = DSem(name, sem(name))
            P.sems[name] = d.sem
            return d

        ds_initS = dsem("d_initS")
        ds_initP = dsem("d_initP")
        ds_initBS = dsem("d_initBS")
        ds_initBP = dsem("d_initBP")
        ds_cv = [dsem(f"d_cv{i}") for i in range(RING)]
        ds_ring = [dsem(f"d_ring{i}") for i in range(RING)]
        ds_scr = [dsem(f"d_scr{i}") for i in range(RING)]
        ds_xT = [dsem(f"d_xT{i}") for i in range(2)]
        ds_xtk = [dsem(f"d_xtk{i}") for i in range(4)]
        ds_out = [dsem(f"d_out{i}") for i in range(4)]

        B = {}

        def bf(name):
            if name not in B:
                B[name] = Buf(name)
            return B[name]

        b_ring = [bf(f"ring{i}") for i in range(RING)]
        b_scr = [bf(f"scr{i}") for i in range(NBLK)]
        b_xT = [bf(f"xT{i}") for i in range(2)]
        b_ya = [bf(f"ya{k}") for k in range(NCH)]
        b_yb = [bf(f"yb{k}") for k in range(NCH)]
        b_mrg = [bf(f"mrg{k}") for k in range(NCH)]
        b_a = [bf(f"a{k}") for k in range(NCH)]
        b_a2m = [bf(f"a2m{k}") for k in range(NCH)]
        b_w = [bf(f"w{k}") for k in range(NCH)]
        b_carry = [bf(f"carry{k}") for k in range(NCH)]
        b_bcarry = [bf(f"bcarry{k}") for k in range(NCH)]
        b_hst = [bf(f"hst{k}") for k in range(NCH)]
        b_gen = [bf(f"gen{i}") for i in range(NGEN)]
        b_xcbf = [bf(f"xcbf{i}") for i in range(4)]
        b_mix = [bf(f"mix{i}") for i in range(2)]
        b_xtk = [bf(f"xtk{i}") for i in range(4)]
        b_junk = bf("junk")
        b_stat = [bf(f"stat{i}") for i in range(4)]
        b_psA = [bf(f"psA{i}") for i in range(4)]
        b_psD = [bf(f"psD{i}") for i in range(2)]
        b_psO = [bf(f"psO{i}") for i in range(2)]
        b_const = bf("const")
        b_constP = bf("constP")
        b_constBP = bf("constBP")
        b_const2 = bf("const2")
        b_constB = bf("constB")
        b_der = bf("der")
        b_tmpv = [bf(f"tmpv{i}") for i in range(6)]

        rr = {"gen": 0, "psA": 0, "psO": 0, "xcbf": 0, "mix": 0}

        free_q = list(range(NGEN))

        def galloc():
            assert free_q, "gen pool exhausted"
            i = free_q.pop(0)
            b = b_gen[i]
            b.live = True
            return gen[i], b

        def gfree(b):
            b.live = False
            free_q.append(b_gen.index(b))

        def psA_alloc():
            i = rr["psA"]
            rr["psA"] = (i + 1) % 4
            return psA[i], b_psA[i]

        def psO_alloc():
            i = rr["psO"]
            rr["psO"] = (i + 1) % 2
            return psO[i], b_psO[i]

        def init_load(eng, dst, src, late=False):
            if late:
                ds = ds_initBS if eng is sp else ds_initBP
            else:
                ds = ds_initS if eng is sp else ds_initP
            P.dma(eng, ds, lambda h, d=dst, s=src: h.dma_start(out=d, in_=s))

        def seal(ds, b):
            tok = (ds.name, 16 * ds.count)
            P.clock[tok] = {ds.name: 16 * ds.count}
            b.w = tok

        init_load(sp, vecs[:], vecs_d)
        init_load(sp, flag[:], flag_d)
        init_load(pool, wxa[:], wxa_d)
        init_load(pool, wg[:], wg_d)
        seal(ds_initS, b_const)
        seal(ds_initP, b_constP)

        def memset(eng, ap, val, b):
            P.op(eng, lambda h, a=ap, v=val: h.memset(a, v), writes=[b])

        memset(pool, zcol[:, 0:1], 0.0, b_const2)
        memset(pool, zcol[:, 1:2], -0.5, b_const2)
        for k in range(NCH):
            memset(pool, carry[:, k, :], 0.0, b_carry[k])
            memset(pool, hstate[:, k:k + 1], 0.0, b_hst[k])

        def vcol(c0):
            return vecs[:, c0:c0 + NCH]

        def dcol(c0):
            return der[:, c0:c0 + NCH]

        def tiny_ts(outap, inap, s1, s2, op0, op1=None, reads=(), writes=()):
            if op1 is None:
                P.op(dve, lambda h: h.tensor_scalar(out=outap, in0=inap, scalar1=s1, scalar2=0.0, op0=op0, op1=ALU.add),
                     reads=reads, writes=writes)
            else:
                P.op(dve, lambda h: h.tensor_scalar(out=outap, in0=inap, scalar1=s1, scalar2=s2, op0=op0, op1=op1),
                     reads=reads, writes=writes)

        def tiny_tt(outap, in0, in1, op, reads=(), writes=()):
            P.op(dve, lambda h: h.tensor_tensor(out=outap, in0=in0, in1=in1, op=op), reads=reads, writes=writes)

        for dc, vc in ((D_HBZA, V_BIN + 8), (D_HBZB, V_BIN + 24), (D_HBGA, V_BIN + 32),
                       (D_HBGB, V_BIN + 40), (D_HBR, V_BR), (D_HBI, V_BI)):
            tiny_ts(dcol(dc), vcol(vc), 0.5, None, ALU.mult, reads=[b_const], writes=[b_der])
        tiny_ts(dcol(D_SC2), vcol(V_PS), 0.5, None, ALU.mult, reads=[b_const], writes=[b_der])
        tiny_tt(dcol(D_BP2), vcol(V_BP), dcol(D_SC2), ALU.mult, reads=[b_const, b_der], writes=[b_der])
        P.op(dve, lambda h: h.tensor_scalar(out=dcol(D_BXBF), in0=vcol(V_BIN + 16), scalar1=flag[:, 0:1], scalar2=zcol[:, 0:1], op0=ALU.mult, op1=ALU.add), reads=[b_const, b_const2], writes=[b_der])
        t_al, t_t, t_s, t_s2, t_p, t_m = tmpv
        bt_al, bt_t, bt_s, bt_s2, bt_p, bt_m = b_tmpv
        P.op(act, lambda h: h.activation(out=t_al[:], in_=vcol(V_LAM), func=AF.Abs), reads=[b_const], writes=[bt_al])
        P.op(act, lambda h: h.activation(out=t_t[:], in_=t_al[:], func=AF.Exp, scale=-1.0), reads=[bt_al], writes=[bt_t])
        tiny_ts(t_m[:], vcol(V_LAM), -1.0, 0.0, ALU.mult, ALU.max, reads=[b_const], writes=[bt_m])
        tiny_ts(t_s[:], t_t[:], 2.0, None, ALU.add, reads=[bt_t], writes=[bt_s])
        P.op(dve, lambda h: h.reciprocal(out=t_s[:], in_=t_s[:]), reads=[bt_s], writes=[bt_s])
        tiny_tt(t_s[:], t_s[:], t_t[:], ALU.mult, reads=[bt_s, bt_t], writes=[bt_s])
        tiny_tt(t_s2[:], t_s[:], t_s[:], ALU.mult, reads=[bt_s], writes=[bt_s2])
        tiny_ts(t_p[:], t_s2[:], 1.0 / 17.0, 1.0 / 15.0, ALU.mult, ALU.add, reads=[bt_s2], writes=[bt_p])
        for den in (13.0, 11.0, 9.0, 7.0, 5.0, 3.0, 1.0):
            tiny_tt(t_p[:], t_p[:], t_s2[:], ALU.mult, reads=[bt_p, bt_s2], writes=[bt_p])
            tiny_ts(t_p[:], t_p[:], 1.0 / den, None, ALU.add, reads=[bt_p], writes=[bt_p])
        tiny_tt(t_p[:], t_p[:], t_s[:], ALU.mult, reads=[bt_p, bt_s], writes=[bt_p])
        P.op(dve, lambda h: h.scalar_tensor_tensor(out=t_p[:], in0=t_p[:], scalar=2.0, in1=t_m[:],
                                                   op0=ALU.mult, op1=ALU.add), reads=[bt_p, bt_m], writes=[bt_p])
        tiny_ts(dcol(D_NSP4), t_p[:], -4.0, None, ALU.mult, reads=[bt_p], writes=[b_der])
        tiny_ts(dcol(D_NSP8), t_p[:], -8.0, None, ALU.mult, reads=[bt_p], writes=[b_der])

        CONST = [b_const, b_constP, b_der, b_const2]

        def vc1(c0, k):
            return vecs[:, c0 + k:c0 + k + 1]

        def dc1(c0, k):
            return der[:, c0 + k:c0 + k + 1]

        def mm_group(ps, bps, lhs_list, rhs_list, reads):
            n = len(lhs_list)
            for i in range(n):
                fn = (lambda h, l=lhs_list[i], r=rhs_list[i], st=(i == 0), sp_=(i == n - 1):
                      h.matmul(ps, l, r, start=st, stop=sp_))
                if i == n - 1:
                    P.op(pe, fn, reads=reads, writes=[bps], skip_own=True)
                else:
                    P.quiet(pe, fn, reads=reads, writes=[bps])

        def load_xT(src, slot):
            P.dma(pool, ds_xT[slot], lambda h: h.dma_start(out=xT[slot][:], in_=src), writes=[b_xT[slot]])

        conv_state = {"n": 0}

        def convert_block():
            b = conv_state["n"]
            if b >= NBLK:
                return
            conv_state["n"] = b + 1
            s = b % RING
            P.dma(pool, ds_cv[s], lambda h: h.dma_start(out=ring[s][:], in_=wst_d[b]), writes=[b_ring[s]])
            P.dma(sp, ds_scr[s], lambda h: h.dma_start(out=wscr[b], in_=ring[s][:]), reads=[b_ring[s]], writes=[b_scr[b]])

        ring_state = {"issued": 0, "used": 0}
        total_blocks = nt * NBLK

        def ring_issue():
            g = ring_state["issued"]
            if g >= total_blocks:
                return
            ring_state["issued"] = g + 1
            s = g % RING
            b = g % NBLK
            P.dma(sp, ds_ring[s], lambda h: h.dma_start(out=ring[s][:], in_=wscr[b]), reads=[b_scr[b]], writes=[b_ring[s]])

        def ring_next(expect_blk):
            g = ring_state["used"]
            assert g % NBLK == expect_blk, (g, expect_blk)
            ring_state["used"] = g + 1
            s = g % RING
            return ring[s], b_ring[s]

        LEAD = 2
        chain_ctx = {}

        def stage_A1(xTs, bxT, k, convert):
            ps, bps = psA_alloc()
            mm_group(ps[:], bps, [wxa[:, k, kk, :] for kk in range(NCH)], [xTs[:, kk, :] for kk in range(NCH)],
                     reads=[bxT] + CONST)
            xa, bxa = galloc()
            bxm = Buf("xa_main")
            P.op(act, lambda h: h.activation(out=xa[:, HX:HX + T], in_=ps[:], func=AF.Identity, bias=vc1(V_BIN, k)),
                 reads=[bps] + CONST, writes=[bxa, bxm])
            xc, bxc = galloc()
            if convert:
                P.op(act, lambda h: h.activation(out=xc[:, 0:T], in_=xa[:, HX:HX + T], func=AF.Identity,
                                                 bias=vc1(V_CB, k), scale=vc1(V_CW + 24, k)),
                     reads=[bxm] + CONST, writes=[bxc])
            P.op(pool, lambda h: h.tensor_copy(out=xa[:, 0:HX], in_=carry[:, k, :]), reads=[b_carry[k]], writes=[bxa])
            P.op(pool, lambda h: h.tensor_copy(out=carry[:, k, :], in_=xa[:, T:T + HX]), reads=[bxa], writes=[b_carry[k]])
            if convert:
                convert_block()
                taps_ = (0, 1, 2)
            else:
                P.op(pool, lambda h: h.tensor_scalar(out=xc[:, 0:T], in0=xa[:, 0:T], scalar1=vc1(V_CW, k), scalar2=vc1(V_CB, k),
                                                     op0=ALU.mult, op1=ALU.add), reads=[bxa] + CONST, writes=[bxc])
                taps_ = (1, 2, 3)
            for tap in taps_:
                P.op(dve, lambda h, tp=tap: h.scalar_tensor_tensor(out=xc[:, 0:T], in0=xa[:, tp:tp + T],
                                                                  scalar=vc1(V_CW + 8 * tp, k), in1=xc[:, 0:T],
                                                                  op0=ALU.mult, op1=ALU.add),
                     reads=[bxa, bxc] + CONST, writes=[bxc])
            gfree(bxa)
            ci = rr["xcbf"]
            rr["xcbf"] = (ci + 1) % len(xcbf)
            cast_eng = pool if convert else dve
            P.op(cast_eng, lambda h: h.tensor_copy(out=xcbf[ci][:], in_=xc[:, 0:T]), reads=[bxc], writes=[b_xcbf[ci]])
            chain_ctx[k] = (xc, bxc, ci)

        def stage_A2(k):
            xc, bxc, ci = chain_ctx.pop(k)
            psr, bpsr = psA_alloc()
            mm_group(psr[:], bpsr, [wg[:, k, 0, :]], [xcbf[ci][:]], reads=[b_xcbf[ci]] + CONST)
            psi, bpsi = psA_alloc()
            mm_group(psi[:], bpsi, [wg[:, k, 1, :]], [xcbf[ci][:]], reads=[b_xcbf[ci]] + CONST)
            tr, btr = galloc()
            ti, bti = galloc()
            P.op(act, lambda h: h.activation(out=tr[:, 0:T], in_=psr[:], func=AF.Tanh, bias=dc1(D_HBR, k), scale=0.5),
                 reads=[bpsr] + CONST, writes=[btr])
            P.op(act, lambda h: h.activation(out=ti[:, 0:T], in_=psi[:], func=AF.Tanh, bias=dc1(D_HBI, k), scale=0.5),
                 reads=[bpsi] + CONST, writes=[bti])
            P.op(act, lambda h: h.activation(out=a_s[:, k, :], in_=tr[:, 0:T], func=AF.Exp, bias=dc1(D_NSP4, k),
                                             scale=dc1(D_NSP4, k)), reads=[btr] + CONST, writes=[b_a[k]])
            P.op(act, lambda h: h.activation(out=a2m[:, k, :], in_=tr[:, 0:T], func=AF.Exp, bias=dc1(D_NSP8, k),
                                             scale=dc1(D_NSP8, k)), reads=[btr] + CONST, writes=[b_a2m[k]])
            gfree(btr)
            P.op(dve, lambda h: h.scalar_tensor_tensor(out=w_s[:, k, :], in0=ti[:, 0:T], scalar=1.0, in1=xc[:, 0:T],
                                                       op0=ALU.add, op1=ALU.mult), reads=[bti, bxc], writes=[b_w[k]])
            gfree(bti)
            gfree(bxc)

        def sqrt_batch(ks):
            for k in ks:
                P.op(act, lambda h, k=k: h.activation(out=a2m[:, k, :], in_=a2m[:, k, :], func=AF.Sqrt,
                                                      bias=0.0625, scale=-0.0625), reads=[b_a2m[k]], writes=[b_a2m[k]])

        def tail(k, u_on_pool=False):
            ue = pool if u_on_pool else dve
            P.op(ue, lambda h: h.tensor_tensor(out=w_s[:, k, :], in0=w_s[:, k, :], in1=a2m[:, k, :], op=ALU.mult),
                 reads=[b_w[k], b_a2m[k]], writes=[b_w[k]])
            P.op(dve, lambda h: h.tensor_tensor_scan(out=a2m[:, k, :], data0=a_s[:, k, :], data1=w_s[:, k, :],
                                                     initial=hstate[:, k:k + 1], op0=ALU.mult, op1=ALU.add),
                 reads=[b_a[k], b_w[k], b_hst[k]], writes=[b_a2m[k]])
            P.op(dve, lambda h: h.tensor_copy(out=hstate[:, k:k + 1], in_=a2m[:, k, T - 1:T]),
                 reads=[b_a2m[k]], writes=[b_hst[k]])

        def chain_step(xTs, bxT, s, convert, tail_prev):
            if s < NCH:
                if tail_prev:
                    tail(s, u_on_pool=True)
                stage_A1(xTs, bxT, s, convert)
            kk = s - LEAD
            if 0 <= kk < NCH:
                stage_A2(kk)

        def late_init_loads():
            init_load(sp, invc[:], invc_d, True)
            init_load(sp, bc3[:], bc3_d, True)
            init_load(pool, wp[:], wp_d, True)
            init_load(pool, wo[:], wo_d, True)
            init_load(pool, xhalo[:], xhaloT, True)
            seal(ds_initBS, b_constB)
            seal(ds_initBP, b_constBP)

        G = nt_pre * NCH
        if nt_pre > 0:
            load_xT(xpreT[0], 0)
            if nt_pre > 1:
                load_xT(xpreT[1], 1)
            else:
                load_xT(xmainT[0], 1)
            late_init_loads()
            for g in range(G + LEAD):
                ga = g - LEAD
                if ga >= NCH:
                    tail(ga % NCH, u_on_pool=False)
                if g < G:
                    t_, k_ = divmod(g, NCH)
                    stage_A1(xT[t_ % 2], b_xT[t_ % 2], k_, True)
                    if k_ == NCH - 1:
                        if t_ + 2 < nt_pre:
                            load_xT(xpreT[t_ + 2], t_ % 2)
                        elif t_ + 2 == nt_pre:
                            load_xT(xmainT[0], t_ % 2)
                if ga >= 0:
                    stage_A2(ga % NCH)
                    if ga % NCH == NCH - 1:
                        sqrt_batch(range(NCH))
            for k in range(NCH):
                tail(k, u_on_pool=False)
        while conv_state["n"] < NBLK:
            convert_block()
        for k in range(NCH):
            P.op(pool, lambda h, k=k: h.tensor_scalar(out=hstate[:, k:k + 1], in0=hstate[:, k:k + 1],
                                                      scalar1=flag[:, 0:1], scalar2=zcol[:, 0:1], op0=ALU.mult, op1=ALU.add),
                 reads=[b_hst[k]] + CONST, writes=[b_hst[k]])
            P.op(pool, lambda h, k=k: h.tensor_scalar(out=carry[:, k, :], in0=carry[:, k, :],
                                                      scalar1=flag[:, 0:1], scalar2=zcol[:, 0:1], op0=ALU.mult, op1=ALU.add),
                 reads=[b_carry[k]] + CONST, writes=[b_carry[k]])

        xslot0 = nt_pre % 2
        if nt_pre == 0:
            load_xT(xmainT[0], xslot0)
            late_init_loads()
        for _ in range(RING):
            ring_issue()

        xtk_state = {"n": 0}
        n_sub = nt * 4

        def xtk_issue():
            i = xtk_state["n"]
            if i >= n_sub:
                return
            xtk_state["n"] = i + 1
            s = i % 4
            P.dma(sp, ds_xtk[s], lambda h: h.dma_start(out=xtk[s][:], in_=xtok[i * 128:(i + 1) * 128, :]),
                  writes=[b_xtk[s]])

        xtk_issue()
        xtk_issue()
        xtk_issue()
        xtk_issue()
        if nt > 1:
            load_xT(xmainT[1], 1 - xslot0)

        def inproj_group(xTs, bxT, blk):
            rb, brb = ring_next(blk)
            ps, bps = psA_alloc()
            mm_group(ps[:], bps, [rb[:, kk, :] for kk in range(NCH)], [xTs[:, kk, :] for kk in range(NCH)],
                     reads=[bxT, brb])
            return ps, bps, rb, brb

        zbs = {}
        mixs = {}

        def phase_B(j, xTs, bxT, k):
            g = k // 2
            win = POOL_WINDOWS[g]
            ps, bps, rb, brb = inproj_group(xTs, bxT, 2 * k)
            if j == 0:
                psh, bpsh = psA_alloc()
                mm_group(psh[:, 0:HB], bpsh, [rb[:, kk, :] for kk in range(NCH)],
                         [xhalo[:, kk, :] for kk in range(NCH)], reads=[brb, b_constB, b_constBP] + CONST)
            ring_issue()
            xb, bxb = galloc()
            P.op(act, lambda h: h.activation(out=xb[:, HB:HB + T], in_=ps[:], func=AF.Identity,
                                             bias=vc1(V_BIN + 16, k)), reads=[bps] + CONST, writes=[bxb])
            if j == 0:
                P.op(act, lambda h: h.activation(out=xb[:, 0:HB], in_=psh[:, 0:HB], func=AF.Identity,
                                                 bias=dc1(D_BXBF, k), scale=flag[:, 0:1]),
                     reads=[bpsh] + CONST, writes=[bxb])
            else:
                P.op(pool, lambda h: h.tensor_copy(out=xb[:, 0:HB], in_=bcarry[:, k, :]),
                     reads=[b_bcarry[k]], writes=[bxb])
            P.op(pool, lambda h: h.tensor_copy(out=bcarry[:, k, :], in_=xb[:, T:T + HB]),
                 reads=[bxb], writes=[b_bcarry[k]])
            ps2, bps2, rb2, brb2 = inproj_group(xTs, bxT, 2 * k + 1)
            ring_issue()
            zb, bzb = galloc()
            tzb, btzb = galloc()
            P.op(act, lambda h: h.activation(out=zb[:, 0:T], in_=ps2[:], func=AF.Identity, bias=vc1(V_BIN + 24, k)),
                 reads=[bps2] + CONST, writes=[bzb])
            P.op(act, lambda h: h.activation(out=tzb[:, 0:T], in_=ps2[:], func=AF.Tanh, bias=dc1(D_HBZB, k), scale=0.5),
                 reads=[bps2] + CONST, writes=[btzb])
            P.op(dve, lambda h: h.scalar_tensor_tensor(out=zb[:, 0:T], in0=tzb[:, 0:T], scalar=1.0, in1=zb[:, 0:T],
                                                       op0=ALU.add, op1=ALU.mult), reads=[btzb, bzb], writes=[bzb])
            gfree(btzb)
            zbs[k] = (zb, bzb)
            cur, bcur = xb, bxb
            lo = 0
            step = 1
            tmp_bufs = []
            while step < win:
                nxt, bnxt = galloc()
                tmp_bufs.append(bnxt)
                lo2 = lo + step
                P.op(pool, lambda h, c=cur, n=nxt, l=lo2, s=step: h.tensor_tensor(
                    out=n[:, l:HB + T], in0=c[:, l:HB + T], in1=c[:, l - s:HB + T - s], op=ALU.add),
                    reads=[bcur], writes=[bnxt])
                cur, bcur, lo = nxt, bnxt, lo2
                step *= 2
            fin = cur
            mi = k % 2
            if mi == 0:
                ms = rr["mix"]
                rr["mix"] = 1 - ms
                mixs[g] = ms
            ms = mixs[g]
            P.op(dve, lambda h: h.scalar_tensor_tensor(
                out=mixbf[ms][:, mi, :], in0=fin[:, HB:HB + T], scalar=1.0 / win, in1=xb[:, HB:HB + T],
                op0=ALU.mult, op1=ALU.subtract), reads=[bcur, bxb], writes=[b_mix[ms]])
            if j == 0:
                fx, bfx = galloc()
                P.op(dve, lambda h: h.tensor_tensor(out=fx[:, 0:HB], in0=fin[:, HB:2 * HB], in1=invc[:, k, :],
                                                    op=ALU.mult), reads=[bcur, b_constB, b_constBP] + CONST, writes=[bfx])
                P.op(dve, lambda h: h.tensor_tensor(out=mixbf[ms][:, mi, 0:HB], in0=fx[:, 0:HB],
                                                    in1=xb[:, HB:2 * HB], op=ALU.subtract),
                     reads=[bfx, bxb], writes=[b_mix[ms]])
                gfree(bfx)
            for tb in tmp_bufs:
                gfree(tb)
            gfree(bxb)

        def pool_out(g, dc):
            ms = mixs[g]
            ko = 2 * g + dc
            pso, bpso = psA_alloc()
            mm_group(pso[:], bpso, [wp[:, g, kc, dc * 128:(dc + 1) * 128] for kc in range(2)],
                     [mixbf[ms][:, kc, :] for kc in range(2)], reads=[b_mix[ms], b_constB, b_constBP] + CONST)
            ybp, bybp = galloc()
            P.op(act, lambda h: h.activation(out=ybp[:, 0:T], in_=pso[:], func=AF.Identity,
                                             bias=dc1(D_BP2, ko), scale=dc1(D_SC2, ko)),
                 reads=[bpso] + CONST, writes=[bybp])
            zq, bzq = zbs[ko]
            P.op(dve, lambda h: h.tensor_tensor(out=yb[:, ko, :], in0=zq[:, 0:T], in1=ybp[:, 0:T], op=ALU.mult),
                 reads=[bzq, bybp], writes=[b_yb[ko]])
            gfree(bybp)
            gfree(bzq)

        def phase_Z(xTs, bxT, k):
            ps, bps, rb, brb = inproj_group(xTs, bxT, 16 + k)
            ring_issue()
            za, bza = galloc()
            tza, btza = galloc()
            P.op(act, lambda h: h.activation(out=za[:, 0:T], in_=ps[:], func=AF.Identity, bias=vc1(V_BIN + 8, k)),
                 reads=[bps] + CONST, writes=[bza])
            P.op(act, lambda h: h.activation(out=tza[:, 0:T], in_=ps[:], func=AF.Tanh, bias=dc1(D_HBZA, k), scale=0.5),
                 reads=[bps] + CONST, writes=[btza])
            P.op(dve, lambda h: h.scalar_tensor_tensor(out=za[:, 0:T], in0=tza[:, 0:T], scalar=1.0, in1=za[:, 0:T],
                                                       op0=ALU.add, op1=ALU.mult), reads=[btza, bza], writes=[bza])
            P.op(dve, lambda h: h.tensor_tensor(out=ya[:, k, :], in0=za[:, 0:T], in1=a2m[:, k, :], op=ALU.mult),
                 reads=[bza, b_a2m[k]], writes=[b_ya[k]])
            gfree(bza)
            gfree(btza)

        def gate_part(xTs, bxT, d, gi):
            ps, bps, rb, brb = inproj_group(xTs, bxT, C_POS[(d, gi)])
            ring_issue()
            tgt, btgt = galloc()
            hb = D_HBGA if gi == 0 else D_HBGB
            P.op(act, lambda h: h.activation(out=tgt[:, 0:T], in_=ps[:], func=AF.Tanh, bias=dc1(hb, d), scale=0.5),
                 reads=[bps] + CONST, writes=[btgt])
            return tgt, btgt

        def proj_part(d, gi, tgt, btgt):
            rb, brb = ring_next(C_POS[(d, 2 + gi)])
            psd, bpsd = psD[gi], b_psD[gi]
            src, bsrc = (ya, b_ya) if gi == 0 else (yb, b_yb)
            mm_group(psd[:], bpsd, [rb[:, kk, :] for kk in range(NCH)], [src[:, kk, :] for kk in range(NCH)],
                     reads=bsrc + [brb])
            ring_issue()
            P.op(dve, lambda h: h.scalar_tensor_tensor(out=tgt[:, 0:T], in0=tgt[:, 0:T], scalar=1.0,
                                                       in1=psd[:], op0=ALU.add, op1=ALU.mult),
                 reads=[btgt, bpsd], writes=[btgt])

        gates_ctx = {}

        def c_gates(xTs, bxT, d):
            t0, bt0 = gate_part(xTs, bxT, d, 0)
            t1, bt1 = gate_part(xTs, bxT, d, 1)
            gates_ctx[d] = (t0, bt0, t1, bt1)

        def c_finish(d):
            t0, bt0, t1, bt1 = gates_ctx.pop(d)
            proj_part(d, 0, t0, bt0)
            proj_part(d, 1, t1, bt1)
            P.op(pool, lambda h: h.tensor_tensor(out=mrg[:, d, :], in0=t0[:, 0:T], in1=t1[:, 0:T], op=ALU.add),
                 reads=[bt0, bt1], writes=[b_mrg[d]])
            gfree(bt0)
            gfree(bt1)

        def d_xpre(si):
            s = si % 4
            xk, bxk = xtk[s], b_xtk[s]
            P.op(pool, lambda h: h.tensor_scalar(out=xk[:], in0=xk[:], scalar1=ALPHA, scalar2=0.0,
                                                 op0=ALU.mult, op1=ALU.add), reads=[bxk], writes=[bxk])
            P.op(pool, lambda h: h.tensor_tensor(out=xk[:], in0=xk[:], in1=bc3[:, 0, :], op=ALU.add),
                 reads=[bxk, b_constB, b_constBP] + CONST, writes=[bxk])

        def d_out_half(si, tt, eh):
            s = si % 4
            xk, bxk = xtk[s], b_xtk[s]
            pso, bpso = psO_alloc()
            mm_group(pso[:], bpso, [mrg[:, kk, tt * 128:(tt + 1) * 128] for kk in range(NCH)],
                     [wo[:, kk, eh * T:(eh + 1) * T] for kk in range(NCH)], reads=b_mrg + [b_constB, b_constBP] + CONST)
            P.op(dve, lambda h: h.scalar_tensor_tensor(
                out=xk[:, eh * T:(eh + 1) * T], in0=pso[:], scalar=0.5, in1=xk[:, eh * T:(eh + 1) * T],
                op0=ALU.mult, op1=ALU.add), reads=[bpso, bxk], writes=[bxk])

        def d_out(si, tt):
            d_out_half(si, tt, 0)
            d_out_half(si, tt, 1)

        def d_stats(si):
            s = si % 4
            z, bz = xtk[s], b_xtk[s]
            st, bst = stat[s], b_stat[s]
            jv = ya[:, 0:2, :]
            P.op(act, lambda h: h.activation(out=jv, in_=z[:].rearrange("p (a b) -> p a b", a=2), func=AF.Identity,
                                             accum_out=st[:, 0:1]), reads=[bz], writes=[b_ya[0], b_ya[1], bst])
            P.op(act, lambda h: h.activation(out=jv, in_=z[:].rearrange("p (a b) -> p a b", a=2), func=AF.Square,
                                             accum_out=st[:, 1:2]), reads=[bz], writes=[b_ya[0], b_ya[1], bst])

        def d_ln(si):
            s = si % 4
            z, bz = xtk[s], b_xtk[s]
            st, bst = stat[s], b_stat[s]
            P.op(dve, lambda h: h.tensor_scalar(out=st[:, 2:3], in0=st[:, 0:1], scalar1=1.0 / D, scalar2=0.0,
                                                op0=ALU.mult, op1=ALU.add), reads=[bst], writes=[bst])
            P.op(dve, lambda h: h.tensor_tensor(out=st[:, 3:4], in0=st[:, 2:3], in1=st[:, 2:3], op=ALU.mult),
                 reads=[bst], writes=[bst])
            P.op(dve, lambda h: h.scalar_tensor_tensor(out=st[:, 4:5], in0=st[:, 1:2], scalar=1.0 / D,
                                                       in1=st[:, 3:4], op0=ALU.mult, op1=ALU.subtract),
                 reads=[bst], writes=[bst])
            P.op(dve, lambda h: h.tensor_scalar(out=st[:, 5:6], in0=st[:, 4:5], scalar1=LN_EPS, scalar2=0.0,
                                                op0=ALU.add, op1=ALU.add), reads=[bst], writes=[bst])
            P.op(pool, lambda h: h.tensor_tensor(out=st[:, 6:7], in0=st[:, 5:6], in1=zcol[:, 1:2], op=ALU.pow),
                 reads=[bst] + CONST, writes=[bst])
            P.op(dve, lambda h: h.scalar_tensor_tensor(out=z[:], in0=z[:], scalar=st[:, 2:3], in1=bc3[:, 1, :],
                                                       op0=ALU.subtract, op1=ALU.mult),
                 reads=[bz, bst, b_constB, b_constBP] + CONST, writes=[bz])
            P.op(dve, lambda h: h.scalar_tensor_tensor(out=z[:], in0=z[:], scalar=st[:, 6:7], in1=bc3[:, 2, :],
                                                       op0=ALU.mult, op1=ALU.add),
                 reads=[bz, bst, b_constB, b_constBP] + CONST, writes=[bz])
            P.dma(sp, ds_out[s], lambda h: h.dma_start(out=out_d[si * 128:(si + 1) * 128, :], in_=z[:]),
                  reads=[bz])

        for s_ in range(NCH + LEAD):
            chain_step(xT[xslot0], b_xT[xslot0], s_, False, False)
        sqrt_batch(range(NCH))
        for k in range(NCH):
            tail(k)

        def b_chunk(jb, xTb, bxTb, k):
            phase_B(jb, xTb, bxTb, k)
            if k % 2 == 1 and k // 2 >= 1:
                pool_out(k // 2 - 1, 0)
                pool_out(k // 2 - 1, 1)

        for k in range(NCH):
            b_chunk(0, xT[xslot0], b_xT[xslot0], k)

        for j in range(nt):
            slot = (xslot0 + j) % 2
            xTs, bxT = xT[slot], b_xT[slot]
            nslot = 1 - slot
            have_next = j + 1 < nt
            cstep = 0
            for k in range(NCH):
                phase_Z(xTs, bxT, k)
                if k == 3:
                    pool_out(3, 0)
                    pool_out(3, 1)
                if have_next and k in (3, 7):
                    chain_step(xT[nslot], b_xT[nslot], cstep, False, False)
                    cstep += 1
            c_gates(xTs, bxT, 0)
            c_gates(xTs, bxT, 1)
            for d in range(NCH):
                c_finish(d)
                if d + 2 < NCH:
                    c_gates(xTs, bxT, d + 2)
                if have_next:
                    if d < 4:
                        for _ in range(2):
                            if cstep < NCH + LEAD:
                                chain_step(xT[nslot], b_xT[nslot], cstep, False, False)
                                cstep += 1
                        if d == 3:
                            assert cstep == NCH + LEAD
                            sqrt_batch(range(NCH))
                    else:
                        tail(2 * (d - 4))
                        tail(2 * (d - 4) + 1)
            if j + 2 < nt:
                load_xT(xmainT[j + 2], slot)
            base = j * 4
            bk = list(range(NCH)) if have_next else []

            def do_b(n):
                for _ in range(n):
                    if bk:
                        b_chunk(j + 1, xT[nslot], b_xT[nslot], bk.pop(0))

            if j > 0:
                xtk_issue()
            d_xpre(base + 0)
            d_xpre(base + 1)
            d_out(base + 0, 0)
            d_out(base + 1, 1)
            d_stats(base + 0)
            do_b(2)
            d_ln(base + 0)
            d_xpre(base + 2)
            d_out(base + 2, 2)
            d_stats(base + 1)
            do_b(2)
            d_ln(base + 1)
            xtk_issue()
            d_xpre(base + 3)
            d_out(base + 3, 3)
            d_stats(base + 2)
            do_b(2)
            d_ln(base + 2)
            xtk_issue()
            d_stats(base + 3)
            do_b(2)
            d_ln(base + 3)
            xtk_issue()
        for s in range(4):
            if ds_out[s].count:
                tok = (ds_out[s].name, 16 * ds_out[s].count)
                P.wait_tok(sp, tok)

        block = es.enter_context(nc.Block())

        @block.sync
        def _(h):
            replay(sp, h)

        @block.gpsimd
        def _(h):
            replay(pool, h)

        @block.scalar
        def _(h):
            replay(act, h)

        @block.vector
        def _(h):
            replay(dve, h)

        @block.tensor
        def _(h):
            replay(pe, h)

    return nc


def _chunked(v):
    return np.ascontiguousarray(v.reshape(NCH, 128).T)


def _wblock(W, col0):
    return np.ascontiguousarray(W[:, col0:col0 + 128].reshape(NCH, 128, 128).transpose(1, 0, 2))


def prepare_shared(inputs):
    f = lambda n: np.asarray(inputs[n], dtype=np.float32)[0]
    w_in = f("w_in")
    wxa = np.stack([_wblock(w_in, e * 128) for e in range(NCH)], axis=1)
    za_blocks = [_wblock(w_in, 1024 + k * 128) for k in range(NCH)]
    order = []
    for k in range(NCH):
        order.append(_wblock(w_in, 2048 + k * 128))
        order.append(_wblock(w_in, 3072 + k * 128))
    wpa = f("w_proj_a")
    wpb = f("w_proj_b")
    tail_blocks = []
    for d, kind in C_ORDER:
        if kind == 0:
            tail_blocks.append(_wblock(w_in, 4096 + d * 128))
        elif kind == 1:
            tail_blocks.append(_wblock(w_in, 5120 + d * 128))
        elif kind == 2:
            tail_blocks.append(_wblock(wpa, d * 128))
        else:
            tail_blocks.append(_wblock(wpb, d * 128))
    wst = np.stack(order + za_blocks + tail_blocks, axis=0)
    wr = f("w_rgate")
    wi = f("w_igate")
    wg = np.zeros((128, NCH, 2, 128), np.float32)
    for k in range(NCH):
        for hh in range(2):
            h = 2 * k + hh
            wg[hh * 64:(hh + 1) * 64, k, 0, hh * 64:(hh + 1) * 64] = wr[h]
            wg[hh * 64:(hh + 1) * 64, k, 1, hh * 64:(hh + 1) * 64] = wi[h]
    wpool = f("w_pool")
    wp = np.ascontiguousarray(wpool.reshape(4, 2, 128, 256).transpose(2, 0, 1, 3))
    wo = np.ascontiguousarray(f("w_out").reshape(NCH, 128, D).transpose(1, 0, 2))
    vecs = np.zeros((128, NV), np.float32)
    b_in = f("b_in")
    for s in range(6):
        vecs[:, V_BIN + 8 * s:V_BIN + 8 * s + 8] = _chunked(b_in[s * 1024:(s + 1) * 1024])
    cw = f("conv_w")
    for tp in range(4):
        vecs[:, V_CW + 8 * tp:V_CW + 8 * tp + 8] = _chunked(cw[tp])
    vecs[:, V_CB:V_CB + 8] = _chunked(f("conv_b"))
    vecs[:, V_BR:V_BR + 8] = _chunked(f("b_rgate"))
    vecs[:, V_BI:V_BI + 8] = _chunked(f("b_igate"))
    vecs[:, V_LAM:V_LAM + 8] = _chunked(f("lru_lambda"))
    vecs[:, V_BP:V_BP + 8] = _chunked(f("b_pool"))
    vecs[:, V_PS:V_PS + 8] = _chunked(f("pool_scale"))
    bc3 = np.ascontiguousarray(np.broadcast_to(
        np.stack([f("b_out"), f("ln_gain"), f("ln_bias")], axis=0)[None], (128, 3, D)))
    return dict(wxa=wxa, wst=wst, wg=wg, wp=wp, wo=wo, vecs=vecs, bc3=bc3)


def _xT_tiles(xs):
    nt = xs.shape[0] // T
    return np.ascontiguousarray(xs.reshape(nt, T, NCH, 128).transpose(0, 3, 2, 1))


def core_inputs(x, b, half, n_half):
    xs_main = x[b, half * n_half:(half + 1) * n_half]
    if half == 1:
        xs_pre = x[b, 0:n_half]
        halo = x[b, n_half - HB:n_half]
    else:
        xs_pre = xs_main
        halo = np.zeros((HB, D), np.float32)
    invc = np.zeros((128, NCH, HB), np.float32)
    for k in range(NCH):
        win = POOL_WINDOWS[k // 2]
        if half == 0:
            cnt = np.minimum(np.arange(HB) + 1, win)
        else:
            cnt = np.full(HB, win)
        invc[:, k, :] = (1.0 / cnt.astype(np.float32))[None, :]
    return dict(
        xpreT=_xT_tiles(xs_pre),
        xmainT=_xT_tiles(xs_main),
        xhaloT=np.ascontiguousarray(halo.reshape(HB, NCH, 128).transpose(2, 1, 0)),
        xtok=np.ascontiguousarray(xs_main),
        flag=np.full((128, 1), float(half), np.float32),
        invc=invc,
    )


_CACHE = {}


def kernel(**inputs):
    x = np.asarray(inputs["x"], dtype=np.float32)
    bsz, seq, _ = x.shape
    n_half = seq // 2
    nt = n_half // T
    key = (nt,)
    if key not in _CACHE:
        _CACHE[key] = build_program(nt, nt)
    nc = _CACHE[key]
    shared = prepare_shared(inputs)
    n_cores = 2 * bsz
    in_maps = []
    for c in range(n_cores):
        m = dict(shared)
        m.update(core_inputs(x, c // 2, c % 2, n_half))
        in_maps.append(m)
    res = run_bass_kernel_spmd(nc, in_maps, core_ids=list(range(n_cores)))
    out = np.empty((bsz, seq, D), np.float32)
    for c in range(n_cores):
        out[c // 2, (c % 2) * n_half:(c % 2 + 1) * n_half] = np.asarray(res.results[c]["out"])
    return out
```

```python
from contextlib import ExitStack

import numpy as np

import concourse.bass as bass
import concourse.mybir as mybir
from concourse.bass_utils import run_bass_kernel_spmd

F32 = mybir.dt.float32
BF16 = mybir.dt.bfloat16
AF = mybir.ActivationFunctionType
ALU = mybir.AluOpType

D = 1024
NCH = 8
T = 512
HX = 3
HB = 16
POOL_WINDOWS = (2, 4, 8, 16)
LN_EPS = 1e-5
ALPHA = 2.0 ** 0.25
NBLK = 56
RING = 6
NGEN = 14

V_BIN = 0
V_CW = 48
V_CB = 80
V_BR = 88
V_BI = 96
V_LAM = 104
V_BP = 112
V_PS = 120
NV = 128
D_HBZA = 0
D_HBZB = 8
D_HBGA = 16
D_HBGB = 24
D_HBR = 32
D_HBI = 40
D_NSP4 = 48
D_NSP8 = 56
D_SC2 = 64
D_BP2 = 72
D_BXBF = 80
ND = 88


def c_block_order():
    order = [(0, 0), (0, 1), (1, 0), (1, 1)]
    for d in range(NCH):
        order += [(d, 2), (d, 3)]
        if d + 2 < NCH:
            order += [(d + 2, 0), (d + 2, 1)]
    return order


C_ORDER = c_block_order()
C_POS = {dk: 24 + i for i, dk in enumerate(C_ORDER)}


class Buf:
    def __init__(self, name):
        self.name = name
        self.w = None
        self.r = []
        self.live = False


class Eng:
    def __init__(self, name, sem):
        self.name = name
        self.sem = sem
        self.n = 0
        self.ops = []
        self.known = {}


class DSem:
    def __init__(self, name, sem):
        self.name = name
        self.sem = sem
        self.count = 0


class Prog:
    def __init__(self):
        self.clock = {}
        self.sems = {}

    def _merge(self, a, b):
        for k, v in b.items():
            if a.get(k, 0) < v:
                a[k] = v

    def _deps(self, reads, writes):
        deps = []
        for b in reads:
            if b.w is not None:
                deps.append(b.w)
        for b in writes:
            if b.w is not None:
                deps.append(b.w)
            deps.extend(b.r)
        return deps

    def _waits(self, eng, deps, skip_own=False):
        for tok in deps:
            s, v = tok
            if skip_own and s == eng.name:
                continue
            if eng.known.get(s, 0) >= v:
                continue
            eng.ops.append(("wait", self.sems[s], v))
            self._merge(eng.known, self.clock[tok])

    def _post(self, tok, reads, writes):
        for b in reads:
            b.r.append(tok)
        for b in writes:
            b.w = tok
            b.r = []

    def op(self, eng, fn, reads=(), writes=(), skip_own=False):
        self._waits(eng, self._deps(reads, writes), skip_own)
        eng.n += 1
        tok = (eng.name, eng.n)
        c = dict(eng.known)
        c[eng.name] = eng.n
        self.clock[tok] = c
        eng.ops.append(("op", fn, eng.sem, 1))
        self._post(tok, reads, writes)
        return tok

    def quiet(self, eng, fn, reads=(), writes=()):
        self._waits(eng, self._deps(reads, writes), True)
        eng.ops.append(("op", fn, None, 0))

    def dma(self, eng, dsem, fn, reads=(), writes=()):
        self._waits(eng, self._deps(reads, writes))
        dsem.count += 1
        tok = (dsem.name, 16 * dsem.count)
        c = dict(eng.known)
        c[dsem.name] = 16 * dsem.count
        self.clock[tok] = c
        eng.ops.append(("op", fn, dsem.sem, 16))
        self._post(tok, reads, writes)
        return tok

    def wait_tok(self, eng, tok):
        self._waits(eng, [tok])


def replay(eng, h):
    for o in eng.ops:
        if o[0] == "wait":
            h.wait_ge(o[1], o[2])
        else:
            ins = o[1](h)
            if o[2] is not None:
                ins.then_inc(o[2], o[3])


def build_program(nt_pre, nt):
    nc = bass.Bass("TRN2", target_bir_lowering=False)
    ntok = nt * T

    def din(name, shape, dt=F32):
        return nc.dram_tensor(name, list(shape), dt, kind="ExternalInput").ap()

    xpreT = din("xpreT", [max(nt_pre, 1), 128, NCH, T])
    xmainT = din("xmainT", [nt, 128, NCH, T])
    xhaloT = din("xhaloT", [128, NCH, HB])
    xtok = din("xtok", [ntok, D])
    flag_d = din("flag", [128, 1])
    invc_d = din("invc", [128, NCH, HB])
    wxa_d = din("wxa", [128, NCH, NCH, 128])
    wst_d = din("wst", [NBLK, 128, NCH, 128])
    wg_d = din("wg", [128, NCH, 2, 128])
    wp_d = din("wp", [128, 4, 2, 256])
    wo_d = din("wo", [128, NCH, D])
    vecs_d = din("vecs", [128, NV])
    bc3_d = din("bc3", [128, 3, D])
    wscr = nc.dram_tensor("wscr", [NBLK, 128, NCH, 128], BF16, kind="Internal").ap()
    out_d = nc.dram_tensor("out", [ntok, D], F32, kind="ExternalOutput").ap()

    es = ExitStack()
    with es:
        def sb(name, shape, dt=F32):
            return es.enter_context(nc.sbuf_tensor(name, list(shape), dt))

        def sem(name):
            return es.enter_context(nc.semaphore(name))

        wxa = sb("wxa_s", [128, NCH, NCH, 128], BF16)
        wg = sb("wg_s", [128, NCH, 2, 128], BF16)
        wp = sb("wp_s", [128, 4, 2, 256], BF16)
        wo = sb("wo_s", [128, NCH, D], BF16)
        vecs = sb("vecs_s", [128, NV])
        der = sb("der_s", [128, ND])
        tmpv = [sb(f"tmpv{i}", [128, NCH]) for i in range(6)]
        bc3 = sb("bc3_s", [128, 3, D])
        flag = sb("flag_s", [128, 1])
        invc = sb("invc_s", [128, NCH, HB])
        xhalo = sb("xhalo_s", [128, NCH, HB], BF16)
        ring = [sb(f"ring{i}", [128, NCH, 128], BF16) for i in range(RING)]
        xT = [sb(f"xT{i}", [128, NCH, T], BF16) for i in range(2)]
        ya = sb("ya_s", [128, NCH, T], BF16)
        yb = sb("yb_s", [128, NCH, T], BF16)
        mrg = sb("mrg_s", [128, NCH, T], BF16)
        a_s = sb("a_s", [128, NCH, T])
        a2m = sb("a2m_s", [128, NCH, T])
        w_s = sb("w_s", [128, NCH, T])
        carry = sb("carry_s", [128, NCH, HX])
        bcarry = sb("bcarry_s", [128, NCH, HB])
        hstate = sb("hstate_s", [128, NCH])
        zcol = sb("zcol_s", [128, 2])
        gen = [sb(f"gen{i}", [128, HB + T]) for i in range(NGEN)]
        xcbf = [sb(f"xcbf{i}", [128, T], BF16) for i in range(4)]
        mixbf = [sb(f"mixbf{i}", [128, 2, T], BF16) for i in range(2)]
        xtk = [sb(f"xtk{i}", [128, D]) for i in range(4)]
        stat = [sb(f"stat{i}", [128, 8]) for i in range(4)]

        psA = [es.enter_context(nc.psum_tensor(f"psA{i}", [128, T], F32)) for i in range(4)]
        psD = [es.enter_context(nc.psum_tensor(f"psD{i}", [128, T], F32)) for i in range(2)]
        psO = [es.enter_context(nc.psum_tensor(f"psO{i}", [128, T], F32)) for i in range(2)]

        P = Prog()
        act = Eng("act", sem("s_act"))
        dve = Eng("dve", sem("s_dve"))
        pool = Eng("pool", sem("s_pool"))
        pe = Eng("pe", sem("s_pe"))
        sp = Eng("sp", None)
        for e in (act, dve, pool, pe):
            P.sems[e.name] = e.sem

        def dsem(name):
            d = DSem(name, sem(name))
            P.sems[name] = d.sem
            return d

        ds_initS = dsem("d_initS")
        ds_initP = dsem("d_initP")
        ds_initBS = dsem("d_initBS")
        ds_initBP = dsem("d_initBP")
        ds_cv = [dsem(f"d_cv{i}") for i in range(RING)]
        ds_wxa = [dsem(f"d_wxa{i}") for i in range(NCH)]
        ds_ring = [dsem(f"d_ring{i}") for i in range(RING)]
        ds_scr = [dsem(f"d_scr{i}") for i in range(RING)]
        ds_xT = [dsem(f"d_xT{i}") for i in range(2)]
        ds_xtk = [dsem(f"d_xtk{i}") for i in range(4)]
        ds_out = [dsem(f"d_out{i}") for i in range(4)]

        B = {}

        def bf(name):
            if name not in B:
                B[name] = Buf(name)
            return B[name]

        b_ring = [bf(f"ring{i}") for i in range(RING)]
        b_scr = [bf(f"scr{i}") for i in range(NBLK)]
        b_xT = [bf(f"xT{i}") for i in range(2)]
        b_ya = [bf(f"ya{k}") for k in range(NCH)]
        b_yb = [bf(f"yb{k}") for k in range(NCH)]
        b_mrg = [bf(f"mrg{k}") for k in range(NCH)]
        b_a = [bf(f"a{k}") for k in range(NCH)]
        b_a2m = [bf(f"a2m{k}") for k in range(NCH)]
        b_w = [bf(f"w{k}") for k in range(NCH)]
        b_carry = [bf(f"carry{k}") for k in range(NCH)]
        b_bcarry = [bf(f"bcarry{k}") for k in range(NCH)]
        b_hst = [bf(f"hst{k}") for k in range(NCH)]
        b_gen = [bf(f"gen{i}") for i in range(NGEN)]
        b_xcbf = [bf(f"xcbf{i}") for i in range(4)]
        b_mix = [bf(f"mix{i}") for i in range(2)]
        b_xtk = [bf(f"xtk{i}") for i in range(4)]
        b_junk = bf("junk")
        b_stat = [bf(f"stat{i}") for i in range(4)]
        b_psA = [bf(f"psA{i}") for i in range(4)]
        b_psD = [bf(f"psD{i}") for i in range(2)]
        b_psO = [bf(f"psO{i}") for i in range(2)]
        b_const = bf("const")
        b_constP = bf("constP")
        b_wxa = [bf(f"wxa{k}") for k in range(NCH)]
        b_constBP = bf("constBP")
        b_const2 = bf("const2")
        b_constB = bf("constB")
        b_der = bf("der")
        b_tmpv = [bf(f"tmpv{i}") for i in range(6)]

        rr = {"gen": 0, "psA": 0, "psO": 0, "xcbf": 0, "mix": 0}

        free_q = list(range(NGEN))

        def galloc():
            assert free_q, "gen pool exhausted"
            i = free_q.pop(0)
            b = b_gen[i]
            b.live = True
            return gen[i], b

        def gfree(b):
            b.live = False
            free_q.append(b_gen.index(b))

        def psA_alloc():
            i = rr["psA"]
            rr["psA"] = (i + 1) % 4
            return psA[i], b_psA[i]

        def psO_alloc():
            i = rr["psO"]
            rr["psO"] = (i + 1) % 2
            return psO[i], b_psO[i]

        def init_load(eng, dst, src, late=False):
            if late:
                ds = ds_initBS if eng is sp else ds_initBP
            else:
                ds = ds_initS if eng is sp else ds_initP
            P.dma(eng, ds, lambda h, d=dst, s=src: h.dma_start(out=d, in_=s))

        def seal(ds, b):
            tok = (ds.name, 16 * ds.count)
            P.clock[tok] = {ds.name: 16 * ds.count}
            b.w = tok

        init_load(sp, vecs[:], vecs_d)
        init_load(sp, flag[:], flag_d)
        first_src = xpreT[0] if nt_pre > 0 else xmainT[0]
        first_slot = 0 if nt_pre > 0 else nt_pre % 2
        P.dma(pool, ds_xT[first_slot], lambda h: h.dma_start(out=xT[first_slot][:], in_=first_src), writes=[b_xT[first_slot]])
        P.dma(pool, ds_wxa[0], lambda h: h.dma_start(out=wxa[:, 0, :, :], in_=wxa_d[:, 0, :, :]), writes=[b_wxa[0]])
        init_load(pool, wg[:], wg_d)
        for k in range(1, NCH):
            P.dma(pool, ds_wxa[k], lambda h, k=k: h.dma_start(out=wxa[:, k, :, :], in_=wxa_d[:, k, :, :]), writes=[b_wxa[k]])
        seal(ds_initS, b_const)
        seal(ds_initP, b_constP)

        def memset(eng, ap, val, b):
            P.op(eng, lambda h, a=ap, v=val: h.memset(a, v), writes=[b])

        memset(pool, zcol[:, 0:1], 0.0, b_const2)
        memset(pool, zcol[:, 1:2], -0.5, b_const2)
        for k in range(NCH):
            memset(pool, carry[:, k, :], 0.0, b_carry[k])
            memset(pool, hstate[:, k:k + 1], 0.0, b_hst[k])

        def vcol(c0):
            return vecs[:, c0:c0 + NCH]

        def dcol(c0):
            return der[:, c0:c0 + NCH]

        def tiny_ts(outap, inap, s1, s2, op0, op1=None, reads=(), writes=()):
            if op1 is None:
                P.op(dve, lambda h: h.tensor_scalar(out=outap, in0=inap, scalar1=s1, scalar2=0.0, op0=op0, op1=ALU.add),
                     reads=reads, writes=writes)
            else:
                P.op(dve, lambda h: h.tensor_scalar(out=outap, in0=inap, scalar1=s1, scalar2=s2, op0=op0, op1=op1),
                     reads=reads, writes=writes)

        def tiny_tt(outap, in0, in1, op, reads=(), writes=()):
            P.op(dve, lambda h: h.tensor_tensor(out=outap, in0=in0, in1=in1, op=op), reads=reads, writes=writes)

        for dc, vc in ((D_HBZA, V_BIN + 8), (D_HBZB, V_BIN + 24), (D_HBGA, V_BIN + 32),
                       (D_HBGB, V_BIN + 40), (D_HBR, V_BR), (D_HBI, V_BI)):
            tiny_ts(dcol(dc), vcol(vc), 0.5, None, ALU.mult, reads=[b_const], writes=[b_der])
        tiny_ts(dcol(D_SC2), vcol(V_PS), 0.5, None, ALU.mult, reads=[b_const], writes=[b_der])
        tiny_tt(dcol(D_BP2), vcol(V_BP), dcol(D_SC2), ALU.mult, reads=[b_const, b_der], writes=[b_der])
        P.op(dve, lambda h: h.tensor_scalar(out=dcol(D_BXBF), in0=vcol(V_BIN + 16), scalar1=flag[:, 0:1], scalar2=zcol[:, 0:1], op0=ALU.mult, op1=ALU.add), reads=[b_const, b_const2], writes=[b_der])
        t_al, t_t, t_s, t_s2, t_p, t_m = tmpv
        bt_al, bt_t, bt_s, bt_s2, bt_p, bt_m = b_tmpv
        P.op(act, lambda h: h.activation(out=t_al[:], in_=vcol(V_LAM), func=AF.Abs), reads=[b_const], writes=[bt_al])
        P.op(act, lambda h: h.activation(out=t_t[:], in_=t_al[:], func=AF.Exp, scale=-1.0), reads=[bt_al], writes=[bt_t])
        tiny_ts(t_m[:], vcol(V_LAM), -1.0, 0.0, ALU.mult, ALU.max, reads=[b_const], writes=[bt_m])
        tiny_ts(t_s[:], t_t[:], 2.0, None, ALU.add, reads=[bt_t], writes=[bt_s])
        P.op(dve, lambda h: h.reciprocal(out=t_s[:], in_=t_s[:]), reads=[bt_s], writes=[bt_s])
        tiny_tt(t_s[:], t_s[:], t_t[:], ALU.mult, reads=[bt_s, bt_t], writes=[bt_s])
        tiny_tt(t_s2[:], t_s[:], t_s[:], ALU.mult, reads=[bt_s], writes=[bt_s2])
        tiny_ts(t_p[:], t_s2[:], 1.0 / 17.0, 1.0 / 15.0, ALU.mult, ALU.add, reads=[bt_s2], writes=[bt_p])
        for den in (13.0, 11.0, 9.0, 7.0, 5.0, 3.0, 1.0):
            tiny_tt(t_p[:], t_p[:], t_s2[:], ALU.mult, reads=[bt_p, bt_s2], writes=[bt_p])
            tiny_ts(t_p[:], t_p[:], 1.0 / den, None, ALU.add, reads=[bt_p], writes=[bt_p])
        tiny_tt(t_p[:], t_p[:], t_s[:], ALU.mult, reads=[bt_p, bt_s], writes=[bt_p])
        P.op(dve, lambda h: h.scalar_tensor_tensor(out=t_p[:], in0=t_p[:], scalar=2.0, in1=t_m[:],
                                                   op0=ALU.mult, op1=ALU.add), reads=[bt_p, bt_m], writes=[bt_p])
        tiny_ts(dcol(D_NSP4), t_p[:], -4.0, None, ALU.mult, reads=[bt_p], writes=[b_der])
        tiny_ts(dcol(D_NSP8), t_p[:], -8.0, None, ALU.mult, reads=[bt_p], writes=[b_der])

        CONST = [b_const, b_constP, b_der, b_const2]

        def vc1(c0, k):
            return vecs[:, c0 + k:c0 + k + 1]

        def dc1(c0, k):
            return der[:, c0 + k:c0 + k + 1]

        def mm_group(ps, bps, lhs_list, rhs_list, reads):
            n = len(lhs_list)
            for i in range(n):
                fn = (lambda h, l=lhs_list[i], r=rhs_list[i], st=(i == 0), sp_=(i == n - 1):
                      h.matmul(ps, l, r, start=st, stop=sp_))
                if i == n - 1:
                    P.op(pe, fn, reads=reads, writes=[bps], skip_own=True)
                else:
                    P.quiet(pe, fn, reads=reads, writes=[bps])

        def load_xT(src, slot):
            P.dma(pool, ds_xT[slot], lambda h: h.dma_start(out=xT[slot][:], in_=src), writes=[b_xT[slot]])

        conv_state = {"n": 0}

        def convert_block():
            b = conv_state["n"]
            if b >= NBLK:
                return
            conv_state["n"] = b + 1
            s = b % RING
            P.dma(pool, ds_cv[s], lambda h: h.dma_start(out=ring[s][:], in_=wst_d[b]), writes=[b_ring[s]])
            P.dma(sp, ds_scr[s], lambda h: h.dma_start(out=wscr[b], in_=ring[s][:]), reads=[b_ring[s]], writes=[b_scr[b]])

        ring_state = {"issued": 0, "used": 0}
        total_blocks = nt * NBLK

        def ring_issue():
            g = ring_state["issued"]
            if g >= total_blocks:
                return
            ring_state["issued"] = g + 1
            s = g % RING
            b = g % NBLK
            P.dma(sp, ds_ring[s], lambda h: h.dma_start(out=ring[s][:], in_=wscr[b]), reads=[b_scr[b]], writes=[b_ring[s]])

        def ring_next(expect_blk):
            g = ring_state["used"]
            assert g % NBLK == expect_blk, (g, expect_blk)
            ring_state["used"] = g + 1
            s = g % RING
            return ring[s], b_ring[s]

        LEAD = 2
        chain_ctx = {}

        def stage_A1(xTs, bxT, k, convert):
            ps, bps = psA_alloc()
            mm_group(ps[:], bps, [wxa[:, k, kk, :] for kk in range(NCH)], [xTs[:, kk, :] for kk in range(NCH)],
                     reads=[bxT, b_wxa[k], b_const])
            xa, bxa = galloc()
            bxm = Buf("xa_main")
            P.op(act, lambda h: h.activation(out=xa[:, HX:HX + T], in_=ps[:], func=AF.Identity, bias=vc1(V_BIN, k)),
                 reads=[bps] + CONST, writes=[bxa, bxm])
            xc, bxc = galloc()
            if convert:
                P.op(act, lambda h: h.activation(out=xc[:, 0:T], in_=xa[:, HX:HX + T], func=AF.Identity,
                                                 bias=vc1(V_CB, k), scale=vc1(V_CW + 24, k)),
                     reads=[bxm] + CONST, writes=[bxc])
            P.op(pool, lambda h: h.tensor_copy(out=xa[:, 0:HX], in_=carry[:, k, :]), reads=[b_carry[k]], writes=[bxa])
            P.op(pool, lambda h: h.tensor_copy(out=carry[:, k, :], in_=xa[:, T:T + HX]), reads=[bxa], writes=[b_carry[k]])
            if convert:
                convert_block()
                taps_ = (0, 1, 2)
            else:
                P.op(pool, lambda h: h.tensor_scalar(out=xc[:, 0:T], in0=xa[:, 0:T], scalar1=vc1(V_CW, k), scalar2=vc1(V_CB, k),
                                                     op0=ALU.mult, op1=ALU.add), reads=[bxa] + CONST, writes=[bxc])
                taps_ = (1, 2, 3)
            for tap in taps_:
                P.op(dve, lambda h, tp=tap: h.scalar_tensor_tensor(out=xc[:, 0:T], in0=xa[:, tp:tp + T],
                                                                  scalar=vc1(V_CW + 8 * tp, k), in1=xc[:, 0:T],
                                                                  op0=ALU.mult, op1=ALU.add),
                     reads=[bxa, bxc] + CONST, writes=[bxc])
            gfree(bxa)
            ci = rr["xcbf"]
            rr["xcbf"] = (ci + 1) % len(xcbf)
            P.op(dve, lambda h: h.tensor_copy(out=xcbf[ci][:], in_=xc[:, 0:T]), reads=[bxc], writes=[b_xcbf[ci]])
            chain_ctx[k] = (xc, bxc, ci)

        def stage_A2(k):
            xc, bxc, ci = chain_ctx.pop(k)
            psr, bpsr = psA_alloc()
            mm_group(psr[:], bpsr, [wg[:, k, 0, :]], [xcbf[ci][:]], reads=[b_xcbf[ci]] + CONST)
            psi, bpsi = psA_alloc()
            mm_group(psi[:], bpsi, [wg[:, k, 1, :]], [xcbf[ci][:]], reads=[b_xcbf[ci]] + CONST)
            tr, btr = galloc()
            ti, bti = galloc()
            P.op(act, lambda h: h.activation(out=tr[:, 0:T], in_=psr[:], func=AF.Tanh, bias=dc1(D_HBR, k), scale=0.5),
                 reads=[bpsr] + CONST, writes=[btr])
            P.op(act, lambda h: h.activation(out=ti[:, 0:T], in_=psi[:], func=AF.Tanh, bias=dc1(D_HBI, k), scale=0.5),
                 reads=[bpsi] + CONST, writes=[bti])
            P.op(act, lambda h: h.activation(out=a_s[:, k, :], in_=tr[:, 0:T], func=AF.Exp, bias=dc1(D_NSP4, k),
                                             scale=dc1(D_NSP4, k)), reads=[btr] + CONST, writes=[b_a[k]])
            P.op(act, lambda h: h.activation(out=a2m[:, k, :], in_=tr[:, 0:T], func=AF.Exp, bias=dc1(D_NSP8, k),
                                             scale=dc1(D_NSP8, k)), reads=[btr] + CONST, writes=[b_a2m[k]])
            gfree(btr)
            P.op(dve, lambda h: h.scalar_tensor_tensor(out=w_s[:, k, :], in0=ti[:, 0:T], scalar=1.0, in1=xc[:, 0:T],
                                                       op0=ALU.add, op1=ALU.mult), reads=[bti, bxc], writes=[b_w[k]])
            gfree(bti)
            gfree(bxc)

        def sqrt_batch(ks):
            for k in ks:
                P.op(act, lambda h, k=k: h.activation(out=a2m[:, k, :], in_=a2m[:, k, :], func=AF.Sqrt,
                                                      bias=0.0625, scale=-0.0625), reads=[b_a2m[k]], writes=[b_a2m[k]])

        def tail(k, u_on_pool=False):
            ue = pool if u_on_pool else dve
            P.op(ue, lambda h: h.tensor_tensor(out=w_s[:, k, :], in0=w_s[:, k, :], in1=a2m[:, k, :], op=ALU.mult),
                 reads=[b_w[k], b_a2m[k]], writes=[b_w[k]])
            P.op(dve, lambda h: h.tensor_tensor_scan(out=a2m[:, k, :], data0=a_s[:, k, :], data1=w_s[:, k, :],
                                                     initial=hstate[:, k:k + 1], op0=ALU.mult, op1=ALU.add),
                 reads=[b_a[k], b_w[k], b_hst[k]], writes=[b_a2m[k]])
            P.op(dve, lambda h: h.tensor_copy(out=hstate[:, k:k + 1], in_=a2m[:, k, T - 1:T]),
                 reads=[b_a2m[k]], writes=[b_hst[k]])

        def chain_step(xTs, bxT, s, convert, tail_prev):
            if s < NCH:
                if tail_prev:
                    tail(s, u_on_pool=True)
                stage_A1(xTs, bxT, s, convert)
            kk = s - LEAD
            if 0 <= kk < NCH:
                stage_A2(kk)

        def late_init_loads():
            init_load(sp, invc[:], invc_d, True)
            init_load(sp, bc3[:], bc3_d, True)
            init_load(pool, wp[:], wp_d, True)
            init_load(pool, wo[:], wo_d, True)
            init_load(pool, xhalo[:], xhaloT, True)
            seal(ds_initBS, b_constB)
            seal(ds_initBP, b_constBP)

        G = nt_pre * NCH
        if nt_pre > 0:
            if nt_pre > 1:
                load_xT(xpreT[1], 1)
            else:
                load_xT(xmainT[0], 1)
            late_init_loads()
            for g in range(G + LEAD):
                ga = g - LEAD
                if ga >= NCH:
                    tail(ga % NCH, u_on_pool=False)
                if g < G:
                    t_, k_ = divmod(g, NCH)
                    stage_A1(xT[t_ % 2], b_xT[t_ % 2], k_, True)
                    if k_ == NCH - 1:
                        if t_ + 2 < nt_pre:
                            load_xT(xpreT[t_ + 2], t_ % 2)
                        elif t_ + 2 == nt_pre:
                            load_xT(xmainT[0], t_ % 2)
                if ga >= 0:
                    stage_A2(ga % NCH)
                    if ga % NCH == NCH - 1:
                        sqrt_batch(range(NCH))
            for k in range(NCH):
                tail(k, u_on_pool=False)
        while conv_state["n"] < NBLK:
            convert_block()
        for k in range(NCH):
            P.op(pool, lambda h, k=k: h.tensor_scalar(out=hstate[:, k:k + 1], in0=hstate[:, k:k + 1],
                                                      scalar1=flag[:, 0:1], scalar2=zcol[:, 0:1], op0=ALU.mult, op1=ALU.add),
                 reads=[b_hst[k]] + CONST, writes=[b_hst[k]])
            P.op(pool, lambda h, k=k: h.tensor_scalar(out=carry[:, k, :], in0=carry[:, k, :],
                                                      scalar1=flag[:, 0:1], scalar2=zcol[:, 0:1], op0=ALU.mult, op1=ALU.add),
                 reads=[b_carry[k]] + CONST, writes=[b_carry[k]])

        xslot0 = nt_pre % 2
        if nt_pre == 0:
            late_init_loads()
        for _ in range(RING):
            ring_issue()

        xtk_state = {"n": 0}
        n_sub = nt * 4

        def xtk_issue():
            i = xtk_state["n"]
            if i >= n_sub:
                return
            xtk_state["n"] = i + 1
            s = i % 4
            P.dma(sp, ds_xtk[s], lambda h: h.dma_start(out=xtk[s][:], in_=xtok[i * 128:(i + 1) * 128, :]),
                  writes=[b_xtk[s]])

        xtk_issue()
        xtk_issue()
        xtk_issue()
        xtk_issue()
        if nt > 1:
            load_xT(xmainT[1], 1 - xslot0)

        def inproj_group(xTs, bxT, blk):
            rb, brb = ring_next(blk)
            ps, bps = psA_alloc()
            mm_group(ps[:], bps, [rb[:, kk, :] for kk in range(NCH)], [xTs[:, kk, :] for kk in range(NCH)],
                     reads=[bxT, brb])
            return ps, bps, rb, brb

        zbs = {}
        mixs = {}

        def phase_B(j, xTs, bxT, k):
            g = k // 2
            win = POOL_WINDOWS[g]
            ps, bps, rb, brb = inproj_group(xTs, bxT, 2 * k)
            if j == 0:
                psh, bpsh = psA_alloc()
                mm_group(psh[:, 0:HB], bpsh, [rb[:, kk, :] for kk in range(NCH)],
                         [xhalo[:, kk, :] for kk in range(NCH)], reads=[brb, b_constB, b_constBP] + CONST)
            ring_issue()
            xb, bxb = galloc()
            P.op(act, lambda h: h.activation(out=xb[:, HB:HB + T], in_=ps[:], func=AF.Identity,
                                             bias=vc1(V_BIN + 16, k)), reads=[bps] + CONST, writes=[bxb])
            if j == 0:
                P.op(act, lambda h: h.activation(out=xb[:, 0:HB], in_=psh[:, 0:HB], func=AF.Identity,
                                                 bias=dc1(D_BXBF, k), scale=flag[:, 0:1]),
                     reads=[bpsh] + CONST, writes=[bxb])
            else:
                P.op(pool, lambda h: h.tensor_copy(out=xb[:, 0:HB], in_=bcarry[:, k, :]),
                     reads=[b_bcarry[k]], writes=[bxb])
            P.op(pool, lambda h: h.tensor_copy(out=bcarry[:, k, :], in_=xb[:, T:T + HB]),
                 reads=[bxb], writes=[b_bcarry[k]])
            ps2, bps2, rb2, brb2 = inproj_group(xTs, bxT, 2 * k + 1)
            ring_issue()
            zb, bzb = galloc()
            tzb, btzb = galloc()
            P.op(act, lambda h: h.activation(out=zb[:, 0:T], in_=ps2[:], func=AF.Identity, bias=vc1(V_BIN + 24, k)),
                 reads=[bps2] + CONST, writes=[bzb])
            P.op(act, lambda h: h.activation(out=tzb[:, 0:T], in_=ps2[:], func=AF.Tanh, bias=dc1(D_HBZB, k), scale=0.5),
                 reads=[bps2] + CONST, writes=[btzb])
            P.op(dve, lambda h: h.scalar_tensor_tensor(out=zb[:, 0:T], in0=tzb[:, 0:T], scalar=1.0, in1=zb[:, 0:T],
                                                       op0=ALU.add, op1=ALU.mult), reads=[btzb, bzb], writes=[bzb])
            gfree(btzb)
            zbs[k] = (zb, bzb)
            cur, bcur = xb, bxb
            lo = 0
            step = 1
            tmp_bufs = []
            while step < win:
                nxt, bnxt = galloc()
                tmp_bufs.append(bnxt)
                lo2 = lo + step
                P.op(pool, lambda h, c=cur, n=nxt, l=lo2, s=step: h.tensor_tensor(
                    out=n[:, l:HB + T], in0=c[:, l:HB + T], in1=c[:, l - s:HB + T - s], op=ALU.add),
                    reads=[bcur], writes=[bnxt])
                cur, bcur, lo = nxt, bnxt, lo2
                step *= 2
            fin = cur
            mi = k % 2
            if mi == 0:
                ms = rr["mix"]
                rr["mix"] = 1 - ms
                mixs[g] = ms
            ms = mixs[g]
            P.op(dve, lambda h: h.scalar_tensor_tensor(
                out=mixbf[ms][:, mi, :], in0=fin[:, HB:HB + T], scalar=1.0 / win, in1=xb[:, HB:HB + T],
                op0=ALU.mult, op1=ALU.subtract), reads=[bcur, bxb], writes=[b_mix[ms]])
            if j == 0:
                fx, bfx = galloc()
                P.op(dve, lambda h: h.tensor_tensor(out=fx[:, 0:HB], in0=fin[:, HB:2 * HB], in1=invc[:, k, :],
                                                    op=ALU.mult), reads=[bcur, b_constB, b_constBP] + CONST, writes=[bfx])
                P.op(dve, lambda h: h.tensor_tensor(out=mixbf[ms][:, mi, 0:HB], in0=fx[:, 0:HB],
                                                    in1=xb[:, HB:2 * HB], op=ALU.subtract),
                     reads=[bfx, bxb], writes=[b_mix[ms]])
                gfree(bfx)
            for tb in tmp_bufs:
                gfree(tb)
            gfree(bxb)

        def pool_out(g, dc):
            ms = mixs[g]
            ko = 2 * g + dc
            pso, bpso = psA_alloc()
            mm_group(pso[:], bpso, [wp[:, g, kc, dc * 128:(dc + 1) * 128] for kc in range(2)],
                     [mixbf[ms][:, kc, :] for kc in range(2)], reads=[b_mix[ms], b_constB, b_constBP] + CONST)
            ybp, bybp = galloc()
            P.op(act, lambda h: h.activation(out=ybp[:, 0:T], in_=pso[:], func=AF.Identity,
                                             bias=dc1(D_BP2, ko), scale=dc1(D_SC2, ko)),
                 reads=[bpso] + CONST, writes=[bybp])
            zq, bzq = zbs[ko]
            P.op(dve, lambda h: h.tensor_tensor(out=yb[:, ko, :], in0=zq[:, 0:T], in1=ybp[:, 0:T], op=ALU.mult),
                 reads=[bzq, bybp], writes=[b_yb[ko]])
            gfree(bybp)
            gfree(bzq)

        def phase_Z(xTs, bxT, k):
            ps, bps, rb, brb = inproj_group(xTs, bxT, 16 + k)
            ring_issue()
            za, bza = galloc()
            tza, btza = galloc()
            P.op(act, lambda h: h.activation(out=za[:, 0:T], in_=ps[:], func=AF.Identity, bias=vc1(V_BIN + 8, k)),
                 reads=[bps] + CONST, writes=[bza])
            P.op(act, lambda h: h.activation(out=tza[:, 0:T], in_=ps[:], func=AF.Tanh, bias=dc1(D_HBZA, k), scale=0.5),
                 reads=[bps] + CONST, writes=[btza])
            P.op(dve, lambda h: h.scalar_tensor_tensor(out=za[:, 0:T], in0=tza[:, 0:T], scalar=1.0, in1=za[:, 0:T],
                                                       op0=ALU.add, op1=ALU.mult), reads=[btza, bza], writes=[bza])
            P.op(dve, lambda h: h.tensor_tensor(out=ya[:, k, :], in0=za[:, 0:T], in1=a2m[:, k, :], op=ALU.mult),
                 reads=[bza, b_a2m[k]], writes=[b_ya[k]])
            gfree(bza)
            gfree(btza)

        def gate_part(xTs, bxT, d, gi):
            ps, bps, rb, brb = inproj_group(xTs, bxT, C_POS[(d, gi)])
            ring_issue()
            tgt, btgt = galloc()
            hb = D_HBGA if gi == 0 else D_HBGB
            P.op(act, lambda h: h.activation(out=tgt[:, 0:T], in_=ps[:], func=AF.Tanh, bias=dc1(hb, d), scale=0.5),
                 reads=[bps] + CONST, writes=[btgt])
            return tgt, btgt

        def proj_part(d, gi, tgt, btgt):
            rb, brb = ring_next(C_POS[(d, 2 + gi)])
            psd, bpsd = psD[gi], b_psD[gi]
            src, bsrc = (ya, b_ya) if gi == 0 else (yb, b_yb)
            mm_group(psd[:], bpsd, [rb[:, kk, :] for kk in range(NCH)], [src[:, kk, :] for kk in range(NCH)],
                     reads=bsrc + [brb])
            ring_issue()
            P.op(dve, lambda h: h.scalar_tensor_tensor(out=tgt[:, 0:T], in0=tgt[:, 0:T], scalar=1.0,
                                                       in1=psd[:], op0=ALU.add, op1=ALU.mult),
                 reads=[btgt, bpsd], writes=[btgt])

        gates_ctx = {}

        def c_gates(xTs, bxT, d):
            t0, bt0 = gate_part(xTs, bxT, d, 0)
            t1, bt1 = gate_part(xTs, bxT, d, 1)
            gates_ctx[d] = (t0, bt0, t1, bt1)

        def c_finish(d):
            t0, bt0, t1, bt1 = gates_ctx.pop(d)
            proj_part(d, 0, t0, bt0)
            proj_part(d, 1, t1, bt1)
            P.op(pool, lambda h: h.tensor_tensor(out=mrg[:, d, :], in0=t0[:, 0:T], in1=t1[:, 0:T], op=ALU.add),
                 reads=[bt0, bt1], writes=[b_mrg[d]])
            gfree(bt0)
            gfree(bt1)

        def d_xpre(si):
            s = si % 4
            xk, bxk = xtk[s], b_xtk[s]
            P.op(pool, lambda h: h.tensor_scalar(out=xk[:], in0=xk[:], scalar1=ALPHA, scalar2=0.0,
                                                 op0=ALU.mult, op1=ALU.add), reads=[bxk], writes=[bxk])
            P.op(pool, lambda h: h.tensor_tensor(out=xk[:], in0=xk[:], in1=bc3[:, 0, :], op=ALU.add),
                 reads=[bxk, b_constB, b_constBP] + CONST, writes=[bxk])

        def d_out_half(si, tt, eh):
            s = si % 4
            xk, bxk = xtk[s], b_xtk[s]
            pso, bpso = psO_alloc()
            mm_group(pso[:], bpso, [mrg[:, kk, tt * 128:(tt + 1) * 128] for kk in range(NCH)],
                     [wo[:, kk, eh * T:(eh + 1) * T] for kk in range(NCH)], reads=b_mrg + [b_constB, b_constBP] + CONST)
            P.op(dve, lambda h: h.scalar_tensor_tensor(
                out=xk[:, eh * T:(eh + 1) * T], in0=pso[:], scalar=0.5, in1=xk[:, eh * T:(eh + 1) * T],
                op0=ALU.mult, op1=ALU.add), reads=[bpso, bxk], writes=[bxk])

        def d_out(si, tt):
            d_out_half(si, tt, 0)
            d_out_half(si, tt, 1)

        def d_stats(si):
            s = si % 4
            z, bz = xtk[s], b_xtk[s]
            st, bst = stat[s], b_stat[s]
            jv = ya[:, 0:2, :]
            P.op(act, lambda h: h.activation(out=jv, in_=z[:].rearrange("p (a b) -> p a b", a=2), func=AF.Identity,
                                             accum_out=st[:, 0:1]), reads=[bz], writes=[b_ya[0], b_ya[1], bst])
            P.op(act, lambda h: h.activation(out=jv, in_=z[:].rearrange("p (a b) -> p a b", a=2), func=AF.Square,
                                             accum_out=st[:, 1:2]), reads=[bz], writes=[b_ya[0], b_ya[1], bst])

        def d_ln(si):
            s = si % 4
            z, bz = xtk[s], b_xtk[s]
            st, bst = stat[s], b_stat[s]
            P.op(dve, lambda h: h.tensor_scalar(out=st[:, 2:3], in0=st[:, 0:1], scalar1=1.0 / D, scalar2=0.0,
                                                op0=ALU.mult, op1=ALU.add), reads=[bst], writes=[bst])
            P.op(dve, lambda h: h.tensor_tensor(out=st[:, 3:4], in0=st[:, 2:3], in1=st[:, 2:3], op=ALU.mult),
                 reads=[bst], writes=[bst])
            P.op(dve, lambda h: h.scalar_tensor_tensor(out=st[:, 4:5], in0=st[:, 1:2], scalar=1.0 / D,
                                                       in1=st[:, 3:4], op0=ALU.mult, op1=ALU.subtract),
                 reads=[bst], writes=[bst])
            P.op(dve, lambda h: h.tensor_scalar(out=st[:, 5:6], in0=st[:, 4:5], scalar1=LN_EPS, scalar2=0.0,
                                                op0=ALU.add, op1=ALU.add), reads=[bst], writes=[bst])
            P.op(pool, lambda h: h.tensor_tensor(out=st[:, 6:7], in0=st[:, 5:6], in1=zcol[:, 1:2], op=ALU.pow),
                 reads=[bst] + CONST, writes=[bst])
            P.op(dve, lambda h: h.scalar_tensor_tensor(out=z[:], in0=z[:], scalar=st[:, 2:3], in1=bc3[:, 1, :],
                                                       op0=ALU.subtract, op1=ALU.mult),
                 reads=[bz, bst, b_constB, b_constBP] + CONST, writes=[bz])
            P.op(dve, lambda h: h.scalar_tensor_tensor(out=z[:], in0=z[:], scalar=st[:, 6:7], in1=bc3[:, 2, :],
                                                       op0=ALU.mult, op1=ALU.add),
                 reads=[bz, bst, b_constB, b_constBP] + CONST, writes=[bz])
            P.dma(sp, ds_out[s], lambda h: h.dma_start(out=out_d[si * 128:(si + 1) * 128, :], in_=z[:]),
                  reads=[bz])

        for s_ in range(NCH + LEAD):
            chain_step(xT[xslot0], b_xT[xslot0], s_, False, False)
        sqrt_batch(range(NCH))
        for k in range(NCH):
            tail(k)

        def b_chunk(jb, xTb, bxTb, k):
            phase_B(jb, xTb, bxTb, k)
            if k % 2 == 1 and k // 2 >= 1:
                pool_out(k // 2 - 1, 0)
                pool_out(k // 2 - 1, 1)

        for k in range(NCH):
            b_chunk(0, xT[xslot0], b_xT[xslot0], k)

        for j in range(nt):
            slot = (xslot0 + j) % 2
            xTs, bxT = xT[slot], b_xT[slot]
            nslot = 1 - slot
            have_next = j + 1 < nt
            cstep = 0
            for k in range(NCH):
                phase_Z(xTs, bxT, k)
                if k == 3:
                    pool_out(3, 0)
                    pool_out(3, 1)
                if have_next and k in (3, 7):
                    chain_step(xT[nslot], b_xT[nslot], cstep, False, False)
                    cstep += 1
            c_gates(xTs, bxT, 0)
            c_gates(xTs, bxT, 1)
            for d in range(NCH):
                c_finish(d)
                if d + 2 < NCH:
                    c_gates(xTs, bxT, d + 2)
                if have_next:
                    if d < 4:
                        for _ in range(2):
                            if cstep < NCH + LEAD:
                                chain_step(xT[nslot], b_xT[nslot], cstep, False, False)
                                cstep += 1
                        if d == 3:
                            assert cstep == NCH + LEAD
                            sqrt_batch(range(NCH))
                    else:
                        tail(2 * (d - 4))
                        tail(2 * (d - 4) + 1)
            if j + 2 < nt:
                load_xT(xmainT[j + 2], slot)
            base = j * 4
            bk = list(range(NCH)) if have_next else []

            def do_b(n):
                for _ in range(n):
                    if bk:
                        b_chunk(j + 1, xT[nslot], b_xT[nslot], bk.pop(0))

            if j > 0:
                xtk_issue()
            d_xpre(base + 0)
            d_xpre(base + 1)
            d_out(base + 0, 0)
            d_out(base + 1, 1)
            d_stats(base + 0)
            do_b(2)
            d_ln(base + 0)
            d_xpre(base + 2)
            d_out(base + 2, 2)
            d_stats(base + 1)
            do_b(2)
            d_ln(base + 1)
            xtk_issue()
            d_xpre(base + 3)
            d_out(base + 3, 3)
            d_stats(base + 2)
            do_b(2)
            d_ln(base + 2)
            xtk_issue()
            d_stats(base + 3)
            do_b(2)
            d_ln(base + 3)
            xtk_issue()
        for s in range(4):
            if ds_out[s].count:
                tok = (ds_out[s].name, 16 * ds_out[s].count)
                P.wait_tok(sp, tok)

        block = es.enter_context(nc.Block())

        @block.sync
        def _(h):
            replay(sp, h)

        @block.gpsimd
        def _(h):
            replay(pool, h)

        @block.scalar
        def _(h):
            replay(act, h)

        @block.vector
        def _(h):
            replay(dve, h)

        @block.tensor
        def _(h):
            replay(pe, h)

    return nc


def _chunked(v):
    return np.ascontiguousarray(v.reshape(NCH, 128).T)


def _wblock(W, col0):
    return np.ascontiguousarray(W[:, col0:col0 + 128].reshape(NCH, 128, 128).transpose(1, 0, 2))


def prepare_shared(inputs):
    f = lambda n: np.asarray(inputs[n], dtype=np.float32)[0]
    w_in = f("w_in")
    wxa = np.stack([_wblock(w_in, e * 128) for e in range(NCH)], axis=1)
    za_blocks = [_wblock(w_in, 1024 + k * 128) for k in range(NCH)]
    order = []
    for k in range(NCH):
        order.append(_wblock(w_in, 2048 + k * 128))
        order.append(_wblock(w_in, 3072 + k * 128))
    wpa = f("w_proj_a")
    wpb = f("w_proj_b")
    tail_blocks = []
    for d, kind in C_ORDER:
        if kind == 0:
            tail_blocks.append(_wblock(w_in, 4096 + d * 128))
        elif kind == 1:
            tail_blocks.append(_wblock(w_in, 5120 + d * 128))
        elif kind == 2:
            tail_blocks.append(_wblock(wpa, d * 128))
        else:
            tail_blocks.append(_wblock(wpb, d * 128))
    wst = np.stack(order + za_blocks + tail_blocks, axis=0)
    wr = f("w_rgate")
    wi = f("w_igate")
    wg = np.zeros((128, NCH, 2, 128), np.float32)
    for k in range(NCH):
        for hh in range(2):
            h = 2 * k + hh
            wg[hh * 64:(hh + 1) * 64, k, 0, hh * 64:(hh + 1) * 64] = wr[h]
            wg[hh * 64:(hh + 1) * 64, k, 1, hh * 64:(hh + 1) * 64] = wi[h]
    wpool = f("w_pool")
    wp = np.ascontiguousarray(wpool.reshape(4, 2, 128, 256).transpose(2, 0, 1, 3))
    wo = np.ascontiguousarray(f("w_out").reshape(NCH, 128, D).transpose(1, 0, 2))
    vecs = np.zeros((128, NV), np.float32)
    b_in = f("b_in")
    for s in range(6):
        vecs[:, V_BIN + 8 * s:V_BIN + 8 * s + 8] = _chunked(b_in[s * 1024:(s + 1) * 1024])
    cw = f("conv_w")
    for tp in range(4):
        vecs[:, V_CW + 8 * tp:V_CW + 8 * tp + 8] = _chunked(cw[tp])
    vecs[:, V_CB:V_CB + 8] = _chunked(f("conv_b"))
    vecs[:, V_BR:V_BR + 8] = _chunked(f("b_rgate"))
    vecs[:, V_BI:V_BI + 8] = _chunked(f("b_igate"))
    vecs[:, V_LAM:V_LAM + 8] = _chunked(f("lru_lambda"))
    vecs[:, V_BP:V_BP + 8] = _chunked(f("b_pool"))
    vecs[:, V_PS:V_PS + 8] = _chunked(f("pool_scale"))
    bc3 = np.ascontiguousarray(np.broadcast_to(
        np.stack([f("b_out"), f("ln_gain"), f("ln_bias")], axis=0)[None], (128, 3, D)))
    return dict(wxa=wxa, wst=wst, wg=wg, wp=wp, wo=wo, vecs=vecs, bc3=bc3)


def _xT_tiles(xs):
    nt = xs.shape[0] // T
    return np.ascontiguousarray(xs.reshape(nt, T, NCH, 128).transpose(0, 3, 2, 1))


def core_inputs(x, b, half, n_half):
    xs_main = x[b, half * n_half:(half + 1) * n_half]
    if half == 1:
        xs_pre = x[b, 0:n_half]
        halo = x[b, n_half - HB:n_half]
    else:
        xs_pre = xs_main
        halo = np.zeros((HB, D), np.float32)
    invc = np.zeros((128, NCH, HB), np.float32)
    for k in range(NCH):
        win = POOL_WINDOWS[k // 2]
        if half == 0:
            cnt = np.minimum(np.arange(HB) + 1, win)
        else:
            cnt = np.full(HB, win)
        invc[:, k, :] = (1.0 / cnt.astype(np.float32))[None, :]
    return dict(
        xpreT=_xT_tiles(xs_pre),
        xmainT=_xT_tiles(xs_main),
        xhaloT=np.ascontiguousarray(halo.reshape(HB, NCH, 128).transpose(2, 1, 0)),
        xtok=np.ascontiguousarray(xs_main),
        flag=np.full((128, 1), float(half), np.float32),
        invc=invc,
    )


_CACHE = {}


def kernel(**inputs):
    x = np.asarray(inputs["x"], dtype=np.float32)
    bsz, seq, _ = x.shape
    n_half = seq // 2
    nt = n_half // T
    key = (nt,)
    if key not in _CACHE:
        _CACHE[key] = build_program(nt, nt)
    nc = _CACHE[key]
    shared = prepare_shared(inputs)
    n_cores = 2 * bsz
    in_maps = []
    for c in range(n_cores):
        m = dict(shared)
        m.update(core_inputs(x, c // 2, c % 2, n_half))
        in_maps.append(m)
    res = run_bass_kernel_spmd(nc, in_maps, core_ids=list(range(n_cores)))
    out = np.empty((bsz, seq, D), np.float32)
    for c in range(n_cores):
        out[c // 2, (c % 2) * n_half:(c % 2 + 1) * n_half] = np.asarray(res.results[c]["out"])
    return out
```
